# Optimizing a Trainium2 kernel written in Bass

```python
import jax, jax.numpy as jnp
from jax import lax
import numpy as np

D_MODEL = 2048
BATCH = 2
SEQ = 8192
DEPTH = 2

A_HEADS = 4
A_HEAD_DIM = 256
A_WIDTH = A_HEADS * A_HEAD_DIM
A_CHUNK = 64
B_GROUPS = 4
B_GROUP_DIM = 256
B_WIDTH = B_GROUPS * B_GROUP_DIM
B_CHUNK = 128
MIX_WIDTH = A_WIDTH + B_WIDTH
EVEN_SPLITS = (A_WIDTH, 2 * A_WIDTH, 3 * A_WIDTH, 4 * A_WIDTH,
               4 * A_WIDTH + A_HEADS, 4 * A_WIDTH + 2 * A_HEADS,
               4 * A_WIDTH + 2 * A_HEADS + B_WIDTH)
EVEN_IN_COLS = 4 * A_WIDTH + 2 * A_HEADS + 2 * B_WIDTH

C_HEADS = 16
C_KV_HEADS = 4
C_GROUP = C_HEADS // C_KV_HEADS
C_HEAD_DIM = 128
IDX_HEADS = 16
IDX_DIM = 64
IDX_ROPE_DIM = 32
TOPK_MAX = 256
Q_BLOCK = 128
ROPE_THETA = 10000.0
ODD_SPLITS = (C_HEADS * C_HEAD_DIM,
              C_HEADS * C_HEAD_DIM + C_KV_HEADS * C_HEAD_DIM,
              C_HEADS * C_HEAD_DIM + 2 * C_KV_HEADS * C_HEAD_DIM,
              C_HEADS * C_HEAD_DIM + 2 * C_KV_HEADS * C_HEAD_DIM + IDX_HEADS * IDX_DIM,
              C_HEADS * C_HEAD_DIM + 2 * C_KV_HEADS * C_HEAD_DIM + IDX_HEADS * IDX_DIM + IDX_DIM)
ODD_IN_COLS = C_HEADS * C_HEAD_DIM + 2 * C_KV_HEADS * C_HEAD_DIM + IDX_HEADS * IDX_DIM + IDX_DIM + IDX_HEADS

FFN_DIM = 5632
CONV_WIDTH = 3
NORM_EPS = 1e-6
N_EVEN = (DEPTH + 1) // 2
N_ODD = DEPTH // 2

kernel_name = 'hybrid_mlstm_sgu_dsa_convffn'


def rmsnorm(x, g):
    xf = x.astype(jnp.float32)
    y = xf * lax.rsqrt(jnp.mean(xf * xf, axis=-1, keepdims=True) + NORM_EPS)
    return (y * g.astype(jnp.float32)).astype(x.dtype)


def layernorm(x, g, b):
    xf = x.astype(jnp.float32)
    mu = jnp.mean(xf, axis=-1, keepdims=True)
    var = jnp.mean(jnp.square(xf - mu), axis=-1, keepdims=True)
    y = (xf - mu) * lax.rsqrt(var + NORM_EPS)
    return (y * g.astype(jnp.float32) + b.astype(jnp.float32)).astype(x.dtype)


def rope(x, pos):
    d = x.shape[-1]
    inv = jnp.power(jnp.float32(ROPE_THETA), -jnp.arange(0, d, 2, dtype=jnp.float32) / d)
    ang = pos.astype(jnp.float32)[:, None] * inv[None, :]
    cos = jnp.cos(ang)[:, None, :]
    sin = jnp.sin(ang)[:, None, :]
    xf = x.astype(jnp.float32)
    x1, x2 = xf[..., : d // 2], xf[..., d // 2:]
    return jnp.concatenate([x1 * cos - x2 * sin, x2 * cos + x1 * sin], axis=-1).astype(x.dtype)


def partial_rope(x, pos):
    return jnp.concatenate([rope(x[..., :IDX_ROPE_DIM], pos), x[..., IDX_ROPE_DIM:]], axis=-1)


def mlstm_chunkwise(q, k, v, i_pre, f_pre):
    Bn, H, S, d = q.shape
    L = A_CHUNK
    nc = S // L
    q = q * (d ** -0.5)
    lf = jax.nn.log_sigmoid(f_pre)

    def to_chunks(a):
        return jnp.moveaxis(a.reshape(Bn, H, nc, L, *a.shape[3:]), 2, 0)

    xs = (to_chunks(q), to_chunks(k), to_chunks(v), to_chunks(i_pre), to_chunks(lf))
    causal = jnp.tril(jnp.ones((L, L), dtype=bool))

    def step(carry, inp):
        C, n, m = carry
        qc, kc, vc, ic, fc = inp
        b = jnp.cumsum(fc, axis=-1)
        dmat = jnp.where(causal, b[..., :, None] - b[..., None, :] + ic[..., None, :], -jnp.inf)
        m_inter = b + m[..., None]
        m_t = jnp.maximum(m_inter, jnp.max(dmat, axis=-1))
        s = jnp.einsum('bhtd,bhsd->bhts', qc, kc) * jnp.exp(dmat - m_t[..., None])
        w_inter = jnp.exp(m_inter - m_t)
        num = jnp.einsum('bhts,bhsd->bhtd', s, vc) + w_inter[..., None] * jnp.einsum('bhtk,bhkv->bhtv', qc, C)
        den = jnp.sum(s, axis=-1) + w_inter * jnp.einsum('bhtk,bhk->bht', qc, n)
        h = num / jnp.maximum(jnp.abs(den), jnp.exp(-m_t))[..., None]
        b_last = b[..., -1]
        g = b_last[..., None] - b + ic
        m_new = jnp.maximum(b_last + m, jnp.max(g, axis=-1))
        decay = jnp.exp(b_last + m - m_new)
        wk = jnp.exp(g - m_new[..., None])[..., None] * kc
        C_new = decay[..., None, None] * C + jnp.einsum('bhsk,bhsv->bhkv', wk, vc)
        n_new = decay[..., None] * n + jnp.sum(wk, axis=2)
        return (C_new, n_new, m_new), h

    init = (jnp.zeros((Bn, H, d, d), jnp.float32), jnp.zeros((Bn, H, d), jnp.float32),
            jnp.zeros((Bn, H), jnp.float32))
    _, hs = lax.scan(step, init, xs)
    return jnp.moveaxis(hs, 0, 2).reshape(Bn, H, S, d)


def even_mixer(xn, w_in, i_bias, f_bias, a_norm, b_norm, spatial, spatial_bias, w_out):
    Bn, S, _ = xn.shape
    proj = xn @ w_in
    q, k, v, o, i_pre, f_pre, u, z = jnp.split(proj, EVEN_SPLITS, axis=-1)

    def heads(t):
        return t.reshape(Bn, S, A_HEADS, A_HEAD_DIM).transpose(0, 2, 1, 3).astype(jnp.float32)
    i_pre = (i_pre + i_bias).astype(jnp.float32).transpose(0, 2, 1)
    f_pre = (f_pre + f_bias).astype(jnp.float32).transpose(0, 2, 1)
    h = mlstm_chunkwise(heads(q), heads(k), heads(v), i_pre, f_pre)
    h = h.transpose(0, 2, 1, 3).astype(xn.dtype)
    h = rmsnorm(h, a_norm) * jax.nn.sigmoid(o).reshape(Bn, S, A_HEADS, A_HEAD_DIM)
    h_a = h.reshape(Bn, S, A_WIDTH)

    nch = S // B_CHUNK
    u = jax.nn.gelu(u)
    z = rmsnorm(jax.nn.gelu(z).reshape(Bn, nch, B_CHUNK, B_GROUPS, B_GROUP_DIM), b_norm)
    w_s = jnp.where(jnp.tril(jnp.ones((B_CHUNK, B_CHUNK), dtype=bool)), spatial, 0.0)
    zmix = jnp.einsum('gts,bcsgd->bctgd', w_s, z) + spatial_bias.T[:, :, None]
    h_b = u * zmix.reshape(Bn, S, B_WIDTH)

    return jnp.concatenate([h_a, h_b], axis=-1) @ w_out


def odd_mixer(xn, pos, w_in, q_norm, k_norm, idx_ln_g, idx_ln_b, w_out):
    Bn, S, _ = xn.shape
    proj = xn @ w_in
    q, k, v, iq, ik, iw = jnp.split(proj, ODD_SPLITS, axis=-1)
    q = rope(rmsnorm(q.reshape(Bn, S, C_HEADS, C_HEAD_DIM), q_norm), pos)
    k = rope(rmsnorm(k.reshape(Bn, S, C_KV_HEADS, C_HEAD_DIM), k_norm), pos)
    v = v.reshape(Bn, S, C_KV_HEADS, C_HEAD_DIM)
    iq = partial_rope(iq.reshape(Bn, S, IDX_HEADS, IDX_DIM), pos)
    ik = partial_rope(layernorm(ik, idx_ln_g, idx_ln_b)[:, :, None, :], pos)[:, :, 0, :]
    iw = iw * (IDX_HEADS ** -0.5 * IDX_DIM ** -0.5)
    topk = min(TOPK_MAX, S // 4)
    nqb = S // Q_BLOCK
    ik32 = ik.astype(jnp.float32)

    qb_all = jnp.moveaxis(q.reshape(Bn, nqb, Q_BLOCK, C_KV_HEADS, C_GROUP, C_HEAD_DIM), 1, 0)
    iqb_all = jnp.moveaxis(iq.reshape(Bn, nqb, Q_BLOCK, IDX_HEADS, IDX_DIM), 1, 0)
    iwb_all = jnp.moveaxis(iw.reshape(Bn, nqb, Q_BLOCK, IDX_HEADS), 1, 0)
    tb_all = pos.reshape(nqb, Q_BLOCK)

    def block(inp):
        qb, iqb, iwb, tb = inp
        sc = jax.nn.relu(jnp.einsum('bqhd,bsd->bqhs', iqb.astype(jnp.float32), ik32))
        sc = jnp.einsum('bqhs,bqh->bqs', sc, iwb.astype(jnp.float32))
        causal = pos[None, :] <= tb[:, None]
        sc = jnp.where(causal[None], sc, -jnp.inf)
        _, idx = lax.top_k(sc, topk)
        valid = idx <= tb[None, :, None]
        ksel = jax.vmap(lambda kk, ii: kk[ii])(k, idx)
        vsel = jax.vmap(lambda vv, ii: vv[ii])(v, idx)
        logits = jnp.einsum('bqhgd,bqkhd->bqhgk', qb.astype(jnp.float32),
                            ksel.astype(jnp.float32)) * (C_HEAD_DIM ** -0.5)
        logits = jnp.where(valid[:, :, None, None, :], logits, -jnp.inf)
        p = jax.nn.softmax(logits, axis=-1)
        ob = jnp.einsum('bqhgk,bqkhd->bqhgd', p, vsel.astype(jnp.float32))
        return ob.astype(qb.dtype)

    out = lax.map(block, (qb_all, iqb_all, iwb_all, tb_all))
    out = jnp.moveaxis(out, 0, 1).reshape(Bn, S, C_HEADS * C_HEAD_DIM)
    return out @ w_out


def conv_ffn(xn, w_up, conv_w, conv_b, w_down):
    h = xn @ w_up
    ch = h.shape[-1]
    h = lax.conv_general_dilated(h, conv_w[:, None, :], window_strides=(1,),
                                 padding=[(CONV_WIDTH - 1, 0)],
                                 dimension_numbers=('NWC', 'WIO', 'NWC'),
                                 feature_group_count=ch) + conv_b
    a, b = jnp.split(h, 2, axis=-1)
    return (jax.nn.silu(a) * b) @ w_down


def setup_inputs(seed: int = 0) -> dict:
    key = jax.random.key(seed)
    ks = jax.random.split(key, 24)
    nrm = jax.random.normal
    f32 = jnp.float32
    D = D_MODEL
    return {
        'x': nrm(ks[0], (BATCH, SEQ, D), f32),
        'mix_norm': 1.0 + 0.02 * nrm(ks[1], (DEPTH, D), f32),
        'even_w_in': nrm(ks[2], (N_EVEN, D, EVEN_IN_COLS), f32) * D ** -0.5,
        'even_i_bias': 0.1 * nrm(ks[3], (N_EVEN, A_HEADS), f32),
        'even_f_bias': jnp.linspace(3.0, 6.0, A_HEADS, dtype=f32)[None, :] + 0.1 * nrm(ks[4], (N_EVEN, A_HEADS), f32),
        'even_a_norm': 1.0 + 0.02 * nrm(ks[5], (N_EVEN, A_HEADS, A_HEAD_DIM), f32),
        'even_b_norm': 1.0 + 0.02 * nrm(ks[6], (N_EVEN, B_GROUPS, B_GROUP_DIM), f32),
        'even_spatial': nrm(ks[7], (N_EVEN, B_GROUPS, B_CHUNK, B_CHUNK), f32) * B_CHUNK ** -0.5,
        'even_spatial_bias': 1.0 + 0.02 * nrm(ks[8], (N_EVEN, B_GROUPS, B_CHUNK), f32),
        'even_w_out': nrm(ks[9], (N_EVEN, MIX_WIDTH, D), f32) * MIX_WIDTH ** -0.5,
        'odd_w_in': nrm(ks[10], (N_ODD, D, ODD_IN_COLS), f32) * D ** -0.5,
        'odd_q_norm': 1.0 + 0.02 * nrm(ks[11], (N_ODD, C_HEAD_DIM), f32),
        'odd_k_norm': 1.0 + 0.02 * nrm(ks[12], (N_ODD, C_HEAD_DIM), f32),
        'odd_idx_ln_g': 1.0 + 0.02 * nrm(ks[13], (N_ODD, IDX_DIM), f32),
        'odd_idx_ln_b': 0.02 * nrm(ks[14], (N_ODD, IDX_DIM), f32),
        'odd_w_out': nrm(ks[15], (N_ODD, C_HEADS * C_HEAD_DIM, D), f32) * (C_HEADS * C_HEAD_DIM) ** -0.5,
        'ffn_norm': 1.0 + 0.02 * nrm(ks[16], (DEPTH, D), f32),
        'ffn_w_up': nrm(ks[17], (DEPTH, D, 2 * FFN_DIM), f32) * D ** -0.5,
        'ffn_conv_w': nrm(ks[18], (DEPTH, CONV_WIDTH, 2 * FFN_DIM), f32) * CONV_WIDTH ** -0.5,
        'ffn_conv_b': 0.02 * nrm(ks[19], (DEPTH, 2 * FFN_DIM), f32),
        'ffn_w_down': nrm(ks[20], (DEPTH, FFN_DIM, D), f32) * FFN_DIM ** -0.5,
    }


def reference(x, mix_norm, even_w_in, even_i_bias, even_f_bias, even_a_norm, even_b_norm,
              even_spatial, even_spatial_bias, even_w_out, odd_w_in, odd_q_norm, odd_k_norm,
              odd_idx_ln_g, odd_idx_ln_b, odd_w_out, ffn_norm, ffn_w_up, ffn_conv_w,
              ffn_conv_b, ffn_w_down):
    pos = jnp.arange(x.shape[1], dtype=jnp.int32)
    h = x
    for layer in range(DEPTH):
        j = layer // 2
        xn = rmsnorm(h, mix_norm[layer])
        if layer % 2 == 0:
            h = h + even_mixer(xn, even_w_in[j], even_i_bias[j], even_f_bias[j], even_a_norm[j],
                               even_b_norm[j], even_spatial[j], even_spatial_bias[j], even_w_out[j])
        else:
            h = h + odd_mixer(xn, pos, odd_w_in[j], odd_q_norm[j], odd_k_norm[j],
                              odd_idx_ln_g[j], odd_idx_ln_b[j], odd_w_out[j])
        h = h + conv_ffn(rmsnorm(h, ffn_norm[layer]), ffn_w_up[layer], ffn_conv_w[layer],
                         ffn_conv_b[layer], ffn_w_down[layer])
    return h
```

```python
import numpy as np
import concourse.bass as bass
import concourse.mybir as mybir
from contextlib import ExitStack
from concourse.bass_utils import run_bass_kernel_spmd

F32 = mybir.dt.float32
BF16 = mybir.dt.bfloat16
I32 = mybir.dt.int32
AF = mybir.ActivationFunctionType
ALU = mybir.AluOpType
AX = mybir.AxisListType

ENGS = ("pe", "act", "dve", "pool", "sp")
CH = 20000


class Sched:
    def __init__(self, nc, es):
        self.nc, self.es = nc, es
        self.eobj = dict(pe=nc.tensor, act=nc.scalar, dve=nc.vector, pool=nc.gpsimd, sp=nc.sync)
        self.prog = {e: [] for e in ENGS}
        self.cnt = {e: 0 for e in ENGS}
        self.sems = {}
        self.waited = {e: {} for e in ENGS}
        self.lastw = {}
        self.readers = {}
        self.dmacnt = {}
        self.dmalast = {}
        self.nsb = 0

    def sb(self, shape, dt, name=None):
        self.nsb += 1
        return self.es.enter_context(self.nc.sbuf_tensor(name or f"sb{self.nsb}", list(shape), dt))

    def ps(self, shape, dt, name=None):
        self.nsb += 1
        return self.es.enter_context(self.nc.psum_tensor(name or f"ps{self.nsb}", list(shape), dt))

    def sem(self, key):
        if key not in self.sems:
            nm = "s_" + "_".join(str(k) for k in (key if isinstance(key, tuple) else (key,)))
            self.sems[key] = self.es.enter_context(self.nc.semaphore(nm))
        return self.sems[key]

    def _deps(self, reads, writes):
        deps = {}
        def add(tok):
            k, v = tok
            if deps.get(k, 0) < v:
                deps[k] = v
        for r in reads:
            if r in self.lastw:
                add(self.lastw[r])
        for w in writes:
            if w in self.lastw:
                add(self.lastw[w])
            for k, v in self.readers.get(w, {}).items():
                add((k, v))
        return deps

    def _commit(self, tok, reads, writes):
        k, v = tok
        for w in writes:
            self.lastw[w] = tok
            self.readers[w] = {}
        for r in reads:
            if r in writes:
                continue
            d = self.readers.setdefault(r, {})
            if d.get(k, 0) < v:
                d[k] = v

    def _waits(self, e, deps, skip_self=False):
        ws = []
        for k, v in deps.items():
            if skip_self and k[0] == e:
                continue
            if self.waited[e].get(k, 0) >= v:
                continue
            self.waited[e][k] = v
            ws.append((k, v))
        return ws

    def op(self, e, fn, reads=(), writes=()):
        deps = self._deps(reads, writes)
        ws = self._waits(e, deps, skip_self=(e == "pe"))
        n = self.cnt[e]
        self.cnt[e] = n + 1
        tok = ((e, n // CH), n % CH + 1)
        self.prog[e].append((ws, fn, (tok[0], 1)))
        self._commit(tok, reads, writes)
        return tok

    def dma(self, q, out, in_, reads=(), writes=(), key=None, **kw):
        key = ("dma", key if key is not None else (writes[0] if writes else reads[0]))
        deps = self._deps(reads, writes)
        if key in self.dmalast:
            k, v = self.dmalast[key]
            if deps.get(k, 0) < v:
                deps[k] = v
        ws = self._waits(q, deps)
        c = self.dmacnt.get(key, 0) + 1
        self.dmacnt[key] = c
        tok = (key, 16 * c)
        self.dmalast[key] = tok
        self.prog[q].append((ws, lambda eng: eng.dma_start(out=out, in_=in_, **kw), (key, 16)))
        self._commit(tok, reads, writes)
        return tok

    def finish(self):
        ws = self._waits("sp", {k: v for (k, v) in self.dmalast.values()})
        self.prog["sp"].append((ws, None, None))

    def emit(self):
        self.finish()
        for key in list(self.dmalast.keys()):
            self.sem(key)
        for e in ENGS:
            for i in range((self.cnt[e] + CH - 1) // CH):
                self.sem((e, i))
        block = self.es.enter_context(self.nc.Block())
        S = self

        def replay(e):
            def body(eng):
                for ws, fn, inc in S.prog[e]:
                    for k, v in ws:
                        eng.wait_ge(S.sems[k], v)
                    if fn is not None:
                        ins = fn(eng)
                        ins.then_inc(S.sems[inc[0]], inc[1])
            return body

        block.tensor(replay("pe"))
        block.scalar(replay("act"))
        block.vector(replay("dve"))
        block.gpsimd(replay("pool"))
        block.sync(replay("sp"))


D = 2048
T = 2048
NDC = 16
F = 5632
NFC = 44
EPS = 1e-6


def rms_stage(S, hsrc, col0, w, hsub, sq, ssps, rstd, ones, gn, xn, xcol0, tag):
    src = hsrc.rearrange("(c p) t -> p c t", p=128)[:, :, col0:col0 + w]
    S.dma("sp", hsub[:, :, :w], src, reads=(tag if isinstance(tag, list) else [("dram", tag)]), writes=["hsub"])
    S.op("act", lambda e: e.activation(out=sq[:, :, :w], in_=hsub[:, :, :w], func=AF.Square),
         reads=["hsub"], writes=["sq"])
    for c in range(NDC):
        S.op("pe", lambda e, c=c: e.matmul(ssps[:, :w], lhsT=ones[:, :], rhs=sq[:, c, :w],
                                            start=(c == 0), stop=(c == NDC - 1)),
             reads=["sq", "ones"], writes=["ssps"])
    S.op("dve", lambda e: e.tensor_scalar(out=rstd[:, :w], in0=ssps[:, :w], scalar1=1.0 / D, scalar2=EPS,
                                          op0=ALU.mult, op1=ALU.add), reads=["ssps"], writes=["rstd"])
    S.op("act", lambda e: e.activation(out=rstd[:, :w], in_=rstd[:, :w], func=AF.Sqrt), reads=["rstd"], writes=["rstd"])
    S.op("dve", lambda e: e.reciprocal(out=rstd[:, :w], in_=rstd[:, :w]), reads=["rstd"], writes=["rstd"])
    for c in range(NDC):
        S.op("dve", lambda e, c=c: e.scalar_tensor_tensor(out=xn[:, c, xcol0:xcol0 + w], in0=hsub[:, c, :w],
                                                         scalar=gn[:, c:c + 1], in1=rstd[:, :w],
                                                         op0=ALU.mult, op1=ALU.mult),
             reads=["hsub", "rstd", "gn"], writes=[("xn", c)])


def build_of(TG=512):
    nc = bass.Bass("TRN2", target_bir_lowering=False)
    hT = nc.dram_tensor("hT", [D, T + 2], F32, kind="ExternalInput").ap()
    mT = nc.dram_tensor("mT", [D, T + 2], BF16, kind="ExternalInput").ap()
    wo = nc.dram_tensor("wo", [D, D], F32, kind="ExternalInput").ap()
    hmid = nc.dram_tensor("hmid", [D, T + 2], F32).ap()
    gn_d = nc.dram_tensor("gn", [128, NDC], F32, kind="ExternalInput").ap()
    wup = nc.dram_tensor("wup", [D, 2 * F], F32, kind="ExternalInput").ap()
    cw_d = nc.dram_tensor("cw", [128, 2 * NFC, 3], F32, kind="ExternalInput").ap()
    cb_d = nc.dram_tensor("cb", [128, 2 * NFC], F32, kind="ExternalInput").ap()
    wdn = nc.dram_tensor("wdn", [F, D], F32, kind="ExternalInput").ap()
    outT = nc.dram_tensor("outT", [D, T], F32, kind="ExternalOutput").ap()
    NH = TG // 512
    NG = T // TG
    with ExitStack() as es:
        S = Sched(nc, es)
        ones = S.sb([128, 128], BF16, "ones")
        gn = S.sb([128, NDC], F32, "gn_sb")
        cw = S.sb([128, 2 * NFC, 3], F32, "cw_sb")
        cb = S.sb([128, 2 * NFC], F32, "cb_sb")
        xn = S.sb([128, NDC, TG + 2], BF16, "xn")
        gT = S.sb([128, NFC, TG], BF16, "gT")
        hsub = S.sb([128, NDC, 258], F32, "hsub")
        sq = S.sb([128, NDC, 258], BF16, "sq")
        rstd = S.sb([128, 258], F32, "rstd")
        wa = [S.sb([128, NDC, 256], BF16, f"wa{i}") for i in range(2)]
        wb = [S.sb([128, NDC, 256], BF16, f"wb{i}") for i in range(2)]
        wd = [S.sb([128, 22, 256], BF16, f"wd{i}") for i in range(2)]
        araw = [S.sb([128, TG + 2], F32, f"araw{i}") for i in range(2)]
        braw = [S.sb([128, TG + 2], F32, f"braw{i}") for i in range(2)]
        ca = [S.sb([128, TG], F32, f"ca{i}") for i in range(2)]
        cbb = [S.sb([128, TG], F32, f"cbb{i}") for i in range(2)]
        sa = [S.sb([128, TG], F32, f"sa{i}") for i in range(2)]
        ptmp = S.sb([128, TG], F32, "ptmp")
        hres = [S.sb([128, TG], F32, f"hres{i}") for i in range(2)]
        osb = [S.sb([128, TG], F32, f"osb{i}") for i in range(2)]
        psA = [S.ps([128, NH, 512], F32, f"psA{i}") for i in range(2)]
        psB = [S.ps([128, NH, 512], F32, f"psB{i}") for i in range(2)]
        psH = [S.ps([128, 512], F32, f"psH{i}") for i in range(2)]
        ssps = S.ps([128, 512], F32, "ssps")
        assert NH == 1

        S.op("pool", lambda e: e.memset(ones[:], 1.0), writes=["ones"])
        S.dma("sp", gn[:], gn_d[:, :], writes=["gn"])
        S.dma("sp", cw[:], cw_d[:, :, :], writes=["cw"])
        S.dma("sp", cb[:], cb_d[:, :], writes=["cb"])
        wup_v = wup.rearrange("(c p) n -> p c n", p=128)
        wdn_v = wdn.rearrange("(c p) n -> p c n", p=128)
        hT_v = hT.rearrange("(c p) t -> p c t", p=128)
        outT_v = outT.rearrange("(c p) t -> p c t", p=128)
        nload = [0]
        mT_v = mT.rearrange("(c p) t -> p c t", p=128)
        wo_v = wo.rearrange("(c p) n -> p c n", p=128)
        hmid_v = hmid.rearrange("(c p) t -> p c t", p=128)
        mth = S.sb([128, NDC, 2], BF16, "mth")
        blocks = [(0, 2)] + [(2 + i * 512, 512) for i in range(T // 512)]
        for bi, (cb0, bw) in enumerate(blocks):
            if bw == 2:
                src, skeys = mth, ["mth"]
                S.dma("sp", mth[:, :, :], mT_v[:, :, cb0:cb0 + bw], reads=[("dram", "mT")], writes=["mth"])
            else:
                src, skeys = gT, [("gT", kc) for kc in range(NDC)]
                S.dma("sp", gT[:, 0:NDC, :], mT_v[:, :, cb0:cb0 + bw], reads=[("dram", "mT")], writes=skeys)
            for dp in range(NDC // 2):
                s_ = nload[0] % 2
                nload[0] += 1
                S.dma("pool", wa[s_][:], wo_v[:, :, dp * 256:(dp + 1) * 256], reads=[("dram", "wo")], writes=[("wa", s_)])
                for dd in range(2):
                    dc = dp * 2 + dd
                    pst, pskey = (psA[dp % 2], ("psA", dp % 2)) if dd == 0 else (psB[dp % 2], ("psB", dp % 2))
                    for kc in range(NDC):
                        S.op("pe", lambda e, pst=pst, s_=s_, kc=kc, dd=dd, src=src, bw=bw: e.matmul(
                            pst[:, 0, :bw], lhsT=wa[s_][:, kc, dd * 128:(dd + 1) * 128], rhs=src[:, kc, :bw],
                            start=(kc == 0), stop=(kc == NDC - 1)), reads=[("wa", s_)] + skeys, writes=[pskey])
                    S.dma("sp", hres[dd][:, :bw], hT_v[:, dc, cb0:cb0 + bw], reads=[("dram", "hT")], writes=[("hres", dd)])
                    S.op("dve", lambda e, dd=dd, pst=pst, bw=bw: e.tensor_tensor(out=osb[dd][:, :bw], in0=pst[:, 0, :bw], in1=hres[dd][:, :bw],
                                                                              op=ALU.add),
                         reads=[pskey, ("hres", dd)], writes=[("osb", dd)])
                    S.dma("sp", hmid_v[:, dc, cb0:cb0 + bw], osb[dd][:, :bw], reads=[("osb", dd)],
                          writes=[("dram", "hmid", bi, dc)], key=("st", dc % 4))
        hmid_keys = [("dram", "hmid", bi, dc) for bi in range(len(blocks)) for dc in range(NDC)]

        for g in range(NG):
            base = g * TG
            c0 = 0
            while c0 < TG + 2:
                w = min(258, TG + 2 - c0)
                rms_stage(S, hmid, base + c0, w, hsub, sq, ssps, rstd, ones, gn, xn, c0, hmid_keys)
                c0 += w
            xn_all = [("xn", c) for c in range(NDC)]
            for j in range(NFC // 2):
                s = j % 2
                S.dma("pool", wa[s][:], wup_v[:, :, j * 256:(j + 1) * 256], reads=[("dram", "wup")], writes=[("wa", s)])
                S.dma("pool", wb[s][:], wup_v[:, :, F + j * 256:F + (j + 1) * 256], reads=[("dram", "wup")], writes=[("wb", s)])
                for ci in range(2):
                    fc = 2 * j + ci
                    p = fc % 2
                    for (wt, wkey, pst, pskey, hoff) in ((wa[s], ("wa", s), psA[p], ("psA", p), 0), (wb[s], ("wb", s), psB[p], ("psB", p), 2)):
                        for c in range(NDC):
                            S.op("pe", lambda e, wt=wt, pst=pst, c=c, ci=ci: e.matmul(
                                pst[:, 0, :], lhsT=wt[:, c, ci * 128:(ci + 1) * 128], rhs=xn[:, c, 2:2 + TG],
                                start=(c == 0), stop=(c == NDC - 1)), reads=[wkey, ("xn", c)], writes=[pskey])
                        for c in range(NDC):
                            S.op("pe", lambda e, wt=wt, c=c, ci=ci, p=p, hoff=hoff: e.matmul(
                                psH[p][:, hoff:hoff + 2], lhsT=wt[:, c, ci * 128:(ci + 1) * 128], rhs=xn[:, c, 0:2],
                                start=(c == 0), stop=(c == NDC - 1)), reads=[wkey, ("xn", c)], writes=[("psH", p, hoff)])
                    S.op("act", lambda e, p=p: e.activation(out=araw[p][:, 2:2 + TG], in_=psA[p][:, 0, :], func=AF.Copy),
                         reads=[("psA", p)], writes=[("araw", p)])
                    S.op("act", lambda e, p=p: e.activation(out=araw[p][:, 0:2], in_=psH[p][:, 0:2], func=AF.Copy),
                         reads=[("psH", p, 0)], writes=[("araw", p)])
                    S.op("act", lambda e, p=p: e.activation(out=braw[p][:, 2:2 + TG], in_=psB[p][:, 0, :], func=AF.Copy),
                         reads=[("psB", p)], writes=[("braw", p)])
                    S.op("act", lambda e, p=p: e.activation(out=braw[p][:, 0:2], in_=psH[p][:, 2:4], func=AF.Copy),
                         reads=[("psH", p, 2)], writes=[("braw", p)])
                    ch = fc
                    S.op("dve", lambda e, p=p, ch=ch: e.tensor_scalar(
                        out=ca[p][:, :], in0=araw[p][:, 2:2 + TG], scalar1=cw[:, ch, 2:3], scalar2=cb[:, ch:ch + 1],
                        op0=ALU.mult, op1=ALU.add), reads=[("araw", p), "cw", "cb"], writes=[("ca", p)])
                    for k in (1, 0):
                        S.op("dve", lambda e, p=p, ch=ch, k=k: e.scalar_tensor_tensor(
                            out=ca[p][:, :], in0=araw[p][:, k:k + TG], scalar=cw[:, ch, k:k + 1], in1=ca[p][:, :],
                            op0=ALU.mult, op1=ALU.add), reads=[("araw", p), ("ca", p), "cw"], writes=[("ca", p)])
                    ch = NFC + fc
                    S.op("pool", lambda e, p=p, ch=ch: e.tensor_scalar(
                        out=cbb[p][:, :], in0=braw[p][:, 2:2 + TG], scalar1=cw[:, ch, 2:3], scalar2=cb[:, ch:ch + 1],
                        op0=ALU.mult, op1=ALU.add), reads=[("braw", p), "cw", "cb"], writes=[("cbb", p)])
                    for k in (1, 0):
                        S.op("pool", lambda e, p=p, ch=ch, k=k: e.tensor_scalar(
                            out=ptmp[:, :], in0=braw[p][:, k:k + TG], scalar1=cw[:, ch, k:k + 1], scalar2=None,
                            op0=ALU.mult), reads=[("braw", p), "cw"], writes=["ptmp"])
                        S.op("pool", lambda e, p=p: e.tensor_tensor(
                            out=cbb[p][:, :], in0=cbb[p][:, :], in1=ptmp[:, :], op=ALU.add),
                            reads=["ptmp", ("cbb", p)], writes=[("cbb", p)])
                    S.op("act", lambda e, p=p: e.activation(out=sa[p][:, :], in_=ca[p][:, :], func=AF.Silu),
                         reads=[("ca", p)], writes=[("sa", p)])
                    S.op("dve", lambda e, p=p, fc=fc: e.tensor_tensor(out=gT[:, fc, :], in0=sa[p][:, :], in1=cbb[p][:, :],
                                                                        op=ALU.mult),
                         reads=[("sa", p), ("cbb", p)], writes=[("gT", fc)])
            for dp in range(NDC // 2):
                for fh in range(2):
                    s = nload[0] % 2
                    nload[0] += 1
                    S.dma("pool", wd[s][:], wdn_v[:, fh * 22:(fh + 1) * 22, dp * 256:(dp + 1) * 256],
                          reads=[("dram", "wdn")], writes=[("wd", s)])
                    for dd in range(2):
                        pst, pskey = (psA[dp % 2], ("psA", dp % 2)) if dd == 0 else (psB[dp % 2], ("psB", dp % 2))
                        for fcc in range(22):
                            fc = fh * 22 + fcc
                            S.op("pe", lambda e, pst=pst, s=s, fcc=fcc, dd=dd, fc=fc, fh=fh: e.matmul(
                                pst[:, 0, :], lhsT=wd[s][:, fcc, dd * 128:(dd + 1) * 128], rhs=gT[:, fc, :],
                                start=(fh == 0 and fcc == 0), stop=(fh == 1 and fcc == 21)),
                                reads=[("wd", s), ("gT", fc)], writes=[pskey])
                for dd in range(2):
                    dc = dp * 2 + dd
                    pst, pskey = (psA[dp % 2], ("psA", dp % 2)) if dd == 0 else (psB[dp % 2], ("psB", dp % 2))
                    S.dma("sp", hres[dd][:, :], hmid_v[:, dc, base + 2:base + 2 + TG], reads=hmid_keys, writes=[("hres", dd)])
                    S.op("dve", lambda e, dd=dd, pst=pst: e.tensor_tensor(out=osb[dd][:, :], in0=pst[:, 0, :], in1=hres[dd][:, :],
                                                                          op=ALU.add),
                         reads=[pskey, ("hres", dd)], writes=[("osb", dd)])
                    S.dma("sp", outT_v[:, dc, g * TG:(g + 1) * TG], osb[dd][:, :], reads=[("osb", dd)], writes=[("dram", "outT", dc, g)], key=("st", dc % 4))
        S.emit()
    return nc


def ffn_inputs(hT_halo, norm_g, w_up, conv_w, conv_b, w_down):
    return {
        "gn": np.ascontiguousarray(norm_g.reshape(NDC, 128).T),
        "wup": np.ascontiguousarray(w_up),
        "cw": np.ascontiguousarray(conv_w.T.reshape(2 * NFC, 128, 3).transpose(1, 0, 2)),
        "cb": np.ascontiguousarray(conv_b.reshape(2 * NFC, 128).T),
        "wdn": np.ascontiguousarray(w_down),
    }


NQH, NKV, HD = 16, 4, 128
NIH, IDD = 16, 64
ODD_COLS = 4176


def build_b0():
    nc = bass.Bass("TRN2", target_bir_lowering=False)
    hT = nc.dram_tensor("hT", [D, T], F32, kind="ExternalInput").ap()
    gn_d = nc.dram_tensor("gn", [128, NDC], F32, kind="ExternalInput").ap()
    w = nc.dram_tensor("w", [D, ODD_COLS], F32, kind="ExternalInput").ap()
    qnb_d = nc.dram_tensor("qnb", [128, 512], F32, kind="ExternalInput").ap()
    knb_d = nc.dram_tensor("knb", [128, 512], F32, kind="ExternalInput").ap()
    lng_d = nc.dram_tensor("lng", [128, 64], F32, kind="ExternalInput").ap()
    lnb_d = nc.dram_tensor("lnb", [128, 64], F32, kind="ExternalInput").ap()
    cs_d = nc.dram_tensor("cs", [T, 2, 64], F32, kind="ExternalInput").ap()
    cs32_d = nc.dram_tensor("cs32", [T, 2, 16], F32, kind="ExternalInput").ap()
    ident_d = nc.dram_tensor("ident", [128, 128], BF16, kind="ExternalInput").ap()
    qT = nc.dram_tensor("qT", [NQH, 128, T], BF16, kind="ExternalOutput").ap()
    kT = nc.dram_tensor("kT", [NKV, 128, T], BF16, kind="ExternalOutput").ap()
    vo = nc.dram_tensor("v", [T, 512], BF16, kind="ExternalOutput").ap()
    iqT = nc.dram_tensor("iqT", [NIH, 64, T], BF16, kind="ExternalOutput").ap()
    ikT = nc.dram_tensor("ikT", [64, T], BF16, kind="ExternalOutput").ap()
    iwo = nc.dram_tensor("iw", [T, 16], F32, kind="ExternalOutput").ap()
    NT = T // 128
    with ExitStack() as es:
        S = Sched(nc, es)
        ones = S.sb([128, 128], BF16, "ones")
        ident = S.sb([128, 128], BF16, "ident_sb")
        gn = S.sb([128, NDC], F32, "gn_sb")
        qnb = S.sb([128, 512], F32, "qnb_sb")
        knb = S.sb([128, 512], F32, "knb_sb")
        lng = S.sb([128, 64], F32, "lng_sb")
        lnb = S.sb([128, 64], F32, "lnb_sb")
        cs = S.sb([128, NT, 2, 64], F32, "cs_sb")
        cs32 = S.sb([128, NT, 2, 16], F32, "cs32_sb")
        xn = S.sb([128, NDC, T], BF16, "xn")
        hsub = S.sb([128, NDC, 256], F32, "hsub")
        sq = S.sb([128, NDC, 256], BF16, "sq")
        rstd = S.sb([128, 256], F32, "rstd")
        wt = [S.sb([128, NDC, 512], BF16, f"wt{i}") for i in range(2)]
        qf = [S.sb([128, 512], F32, f"qf{i}") for i in range(2)]
        t1 = S.sb([128, 256], F32, "t1")
        t2 = S.sb([128, 256], F32, "t2")
        qr = [S.sb([128, 512], BF16, f"qr{i}") for i in range(2)]
        junk = S.sb([128, 128], F32, "junk")
        st = [S.sb([128, 8], F32, f"st{i}") for i in range(2)]
        trs = [S.sb([128, 8, 128], BF16, f"trs{i}") for i in range(2)]
        iws = [S.sb([128, 16], F32, f"iws{i}") for i in range(2)]
        psP = [S.ps([128, 512], F32, f"psP{i}") for i in range(3)]
        psT = [S.ps([128, 8, 128], BF16, f"psT{i}") for i in range(2)]
        ssps = S.ps([128, 512], F32, "ssps")

        S.op("pool", lambda e: e.memset(ones[:], 1.0), writes=["ones"])
        for (sbt, dt_, k) in ((gn, gn_d, "gn"), (qnb, qnb_d, "qnb"), (knb, knb_d, "knb"), (lng, lng_d, "lng"),
                              (lnb, lnb_d, "lnb"), (ident, ident_d, "ident")):
            S.dma("sp", sbt[:], dt_[:, :], writes=[k])
        S.dma("sp", cs[:], cs_d.rearrange("(n p) a d -> p n a d", p=128), writes=["cs"])
        S.dma("sp", cs32[:], cs32_d.rearrange("(n p) a d -> p n a d", p=128), writes=["cs32"])
        w_v = w.rearrange("(c p) n -> p c n", p=128)
        for i in range(T // 256):
            rms_stage(S, hT, i * 256, 256, hsub, sq, ssps, rstd, ones, gn, xn, i * 256, "hT")
        tiles = [(i * 512, 512, "q", i) for i in range(4)] + [(2048, 512, "k", 0), (2560, 512, "v", 0),
                                                             (3072, 512, "iq", 0), (3584, 512, "iq", 1), (4096, 80, "ik", 0)]
        n = [0, 0, 0]

        def rope(xv, ov, cosv, sinv, shp, rk, wk):
            tv1 = t1[:, :shp[0] * shp[1]].rearrange("p (h d) -> p h d", h=shp[0])
            tv2 = t2[:, :shp[0] * shp[1]].rearrange("p (h d) -> p h d", h=shp[0])
            x1, x2 = xv[:, :, 0, :], xv[:, :, 1, :]
            S.op("dve", lambda e: e.tensor_tensor(out=tv1, in0=x1, in1=cosv, op=ALU.mult), reads=rk, writes=["t1"])
            S.op("pool", lambda e: e.tensor_tensor(out=tv2, in0=x2, in1=sinv, op=ALU.mult), reads=rk, writes=["t2"])
            S.op("dve", lambda e: e.tensor_tensor(out=ov[:, :, 0, :], in0=tv1, in1=tv2, op=ALU.subtract), reads=["t1", "t2"], writes=wk)
            S.op("dve", lambda e: e.tensor_tensor(out=tv1, in0=x2, in1=cosv, op=ALU.mult), reads=rk + wk, writes=["t1"])
            S.op("pool", lambda e: e.tensor_tensor(out=tv2, in0=x1, in1=sinv, op=ALU.mult), reads=rk + wk, writes=["t2"])
            S.op("dve", lambda e: e.tensor_tensor(out=ov[:, :, 1, :], in0=tv1, in1=tv2, op=ALU.add), reads=["t1", "t2"], writes=wk)

        for (col0, cw_, kind, idx) in tiles:
            s = n[0] % 2
            n[0] += 1
            S.dma("pool", wt[s][:, :, :cw_], w_v[:, :, col0:col0 + cw_], reads=[("dram", "w")], writes=[("wt", s)])
            for tt in range(NT):
                pb = n[1] % 3
                n[1] += 1
                ps = psP[pb]
                for c in range(NDC):
                    S.op("pe", lambda e, ps=ps, c=c, tt=tt, s=s, cw_=cw_: e.matmul(
                        ps[:, :cw_], lhsT=xn[:, c, tt * 128:(tt + 1) * 128], rhs=wt[s][:, c, :cw_],
                        start=(c == 0), stop=(c == NDC - 1)), reads=[("wt", s), ("xn", c)], writes=[("psP", pb)])
                b = n[2] % 2
                n[2] += 1
                tok = slice(tt * 128, (tt + 1) * 128)
                if kind in ("q", "k"):
                    nb = qnb if kind == "q" else knb
                    S.op("pool", lambda e, b=b: e.memset(st[b][:], 0.0), writes=[("st", b)])
                    for h in range(4):
                        S.op("act", lambda e, ps=ps, h=h, b=b: e.activation(out=junk[:, :], in_=ps[:, h * 128:(h + 1) * 128],
                                                                         func=AF.Square, accum_out=st[b][:, h:h + 1]),
                             reads=[("psP", pb), ("st", b)], writes=["junk", ("st", b, h)])
                    skeys = [("st", b, h) for h in range(4)]
                    S.op("dve", lambda e, b=b: e.tensor_scalar(out=st[b][:, 4:8], in0=st[b][:, 0:4], scalar1=1.0 / 128, scalar2=EPS,
                                                               op0=ALU.mult, op1=ALU.add), reads=skeys, writes=[("st2", b)])
                    S.op("act", lambda e, b=b: e.activation(out=st[b][:, 4:8], in_=st[b][:, 4:8], func=AF.Sqrt),
                         reads=[("st2", b)], writes=[("st2", b)])
                    S.op("dve", lambda e, b=b: e.reciprocal(out=st[b][:, 4:8], in_=st[b][:, 4:8]), reads=[("st2", b)], writes=[("st2", b)])
                    for h in range(4):
                        S.op("dve", lambda e, ps=ps, h=h, b=b, nb=nb: e.scalar_tensor_tensor(
                            out=qf[b][:, h * 128:(h + 1) * 128], in0=ps[:, h * 128:(h + 1) * 128], scalar=st[b][:, 4 + h:5 + h],
                            in1=nb[:, h * 128:(h + 1) * 128], op0=ALU.mult, op1=ALU.mult),
                            reads=[("psP", pb), ("st2", b), "qnb", "knb"], writes=[("qf", b, h)])
                    xv = qf[b][:, :].rearrange("p (h a d) -> p h a d", h=4, a=2)
                    ov = qr[b][:, :].rearrange("p (h a d) -> p h a d", h=4, a=2)
                    cosv = cs[:, tt, 0, :].unsqueeze(1).to_broadcast([128, 4, 64])
                    sinv = cs[:, tt, 1, :].unsqueeze(1).to_broadcast([128, 4, 64])
                    rope(xv, ov, cosv, sinv, (4, 64), [("qf", b, h) for h in range(4)] + ["cs"], [("qr", b)])
                    for h in range(4):
                        S.op("pe", lambda e, h=h, b=b: e.transpose(out=psT[b][:, h, :], in_=qr[b][:, h * 128:(h + 1) * 128], identity=ident[:, :]),
                             reads=[("qr", b), "ident"], writes=[("psT", b)])
                    S.op("act", lambda e, b=b: e.activation(out=trs[b][:, 0:4, :], in_=psT[b][:, 0:4, :], func=AF.Copy),
                         reads=[("psT", b)], writes=[("trs", b)])
                    dst = (qT[idx * 4:(idx + 1) * 4, :, tok] if kind == "q" else kT[:, :, tok]).rearrange("h d t -> d h t")
                    S.dma("sp", dst, trs[b][:, 0:4, :], reads=[("trs", b)], writes=[("dram", kind, idx, tt)], key=("st", tt % 4))
                elif kind == "v":
                    S.op("act", lambda e, ps=ps, b=b: e.activation(out=qr[b][:, :], in_=ps[:, :], func=AF.Copy),
                         reads=[("psP", pb)], writes=[("qr", b)])
                    S.dma("sp", vo[tok, :], qr[b][:, :], reads=[("qr", b)], writes=[("dram", "v", tt)], key=("st", tt % 4))
                elif kind == "iq":
                    S.op("act", lambda e, ps=ps, b=b: e.activation(out=qf[b][:, :], in_=ps[:, :], func=AF.Copy),
                         reads=[("psP", pb)], writes=[("qf", b, 0)])
                    x4 = qf[b][:, :].rearrange("p (h d) -> p h d", h=8)
                    o4 = qr[b][:, :].rearrange("p (h d) -> p h d", h=8)
                    S.op("pool", lambda e, x4=x4, o4=o4: e.tensor_copy(out=o4[:, :, 32:64], in_=x4[:, :, 32:64]),
                         reads=[("qf", b, 0)], writes=[("qr", b, "hi")])
                    xv = x4[:, :, 0:32].rearrange("p h (a d) -> p h a d", a=2)
                    ov = o4[:, :, 0:32].rearrange("p h (a d) -> p h a d", a=2)
                    cosv = cs32[:, tt, 0, :].unsqueeze(1).to_broadcast([128, 8, 16])
                    sinv = cs32[:, tt, 1, :].unsqueeze(1).to_broadcast([128, 8, 16])
                    rope(xv, ov, cosv, sinv, (8, 16), [("qf", b, 0), "cs32"], [("qr", b)])
                    for h in range(8):
                        S.op("pe", lambda e, h=h, b=b: e.transpose(out=psT[b][0:64, h, :], in_=qr[b][:, h * 64:(h + 1) * 64], identity=ident[:, :]),
                             reads=[("qr", b), ("qr", b, "hi"), "ident"], writes=[("psT", b)])
                    S.op("act", lambda e, b=b: e.activation(out=trs[b][0:64, :, :], in_=psT[b][0:64, :, :], func=AF.Copy),
                         reads=[("psT", b)], writes=[("trs", b)])
                    S.dma("sp", iqT[idx * 8:(idx + 1) * 8, :, tok].rearrange("h d t -> d h t"), trs[b][0:64, :, :],
                          reads=[("trs", b)], writes=[("dram", "iq", idx, tt)], key=("st", tt % 4))
                else:
                    S.op("pool", lambda e, b=b: e.memset(st[b][:], 0.0), writes=[("st", b)])
                    S.op("act", lambda e, ps=ps, b=b: e.activation(out=qf[b][:, 0:64], in_=ps[:, 0:64], func=AF.Copy,
                                                                accum_out=st[b][:, 0:1]), reads=[("psP", pb), ("st", b)], writes=[("qf", b, 0), ("st", b, 0)])
                    S.op("act", lambda e, ps=ps, b=b: e.activation(out=junk[:, 0:64], in_=ps[:, 0:64], func=AF.Square,
                                                                accum_out=st[b][:, 1:2]), reads=[("psP", pb), ("st", b)], writes=["junk", ("st", b, 1)])
                    S.op("act", lambda e, ps=ps, b=b: e.activation(out=iws[b][:, :], in_=ps[:, 64:80], func=AF.Copy, scale=1.0 / 32),
                         reads=[("psP", pb)], writes=[("iws", b)])
                    S.dma("sp", iwo[tok, :], iws[b][:, :], reads=[("iws", b)], writes=[("dram", "iw", tt)], key=("st", tt % 4))
                    sk = [("st", b, 0), ("st", b, 1)]
                    S.op("dve", lambda e, b=b: e.tensor_scalar(out=st[b][:, 2:4], in0=st[b][:, 0:2], scalar1=1.0 / 64, scalar2=None, op0=ALU.mult),
                         reads=sk, writes=[("st2", b)])
                    S.op("dve", lambda e, b=b: e.tensor_tensor(out=st[b][:, 4:5], in0=st[b][:, 2:3], in1=st[b][:, 2:3], op=ALU.mult),
                         reads=[("st2", b)], writes=[("st3", b)])
                    S.op("dve", lambda e, b=b: e.tensor_tensor(out=st[b][:, 5:6], in0=st[b][:, 3:4], in1=st[b][:, 4:5], op=ALU.subtract),
                         reads=[("st2", b), ("st3", b)], writes=[("st4", b)])
                    S.op("dve", lambda e, b=b: e.tensor_scalar(out=st[b][:, 5:6], in0=st[b][:, 5:6], scalar1=EPS, scalar2=None, op0=ALU.add),
                         reads=[("st4", b)], writes=[("st4", b)])
                    S.op("act", lambda e, b=b: e.activation(out=st[b][:, 5:6], in_=st[b][:, 5:6], func=AF.Sqrt), reads=[("st4", b)], writes=[("st4", b)])
                    S.op("dve", lambda e, b=b: e.reciprocal(out=st[b][:, 5:6], in_=st[b][:, 5:6]), reads=[("st4", b)], writes=[("st4", b)])
                    S.op("dve", lambda e, b=b: e.tensor_scalar(out=qf[b][:, 0:64], in0=qf[b][:, 0:64], scalar1=st[b][:, 2:3], scalar2=st[b][:, 5:6],
                                                               op0=ALU.subtract, op1=ALU.mult), reads=[("qf", b, 0), ("st2", b), ("st4", b)], writes=[("qf", b, 0)])
                    S.op("dve", lambda e, b=b: e.tensor_tensor(out=qf[b][:, 0:64], in0=qf[b][:, 0:64], in1=lng[:, :], op=ALU.mult),
                         reads=[("qf", b, 0), "lng"], writes=[("qf", b, 0)])
                    S.op("dve", lambda e, b=b: e.tensor_tensor(out=qf[b][:, 0:64], in0=qf[b][:, 0:64], in1=lnb[:, :], op=ALU.add),
                         reads=[("qf", b, 0), "lnb"], writes=[("qf", b, 0)])
                    x4 = qf[b][:, 0:64].rearrange("p (h d) -> p h d", h=1)
                    o4 = qr[b][:, 0:64].rearrange("p (h d) -> p h d", h=1)
                    S.op("pool", lambda e, x4=x4, o4=o4: e.tensor_copy(out=o4[:, :, 32:64], in_=x4[:, :, 32:64]),
                         reads=[("qf", b, 0)], writes=[("qr", b, "hi")])
                    xv = x4[:, :, 0:32].rearrange("p h (a d) -> p h a d", a=2)
                    ov = o4[:, :, 0:32].rearrange("p h (a d) -> p h a d", a=2)
                    cosv = cs32[:, tt, 0, :].unsqueeze(1)
                    sinv = cs32[:, tt, 1, :].unsqueeze(1)
                    rope(xv, ov, cosv, sinv, (1, 16), [("qf", b, 0), "cs32"], [("qr", b)])
                    S.op("pe", lambda e, b=b: e.transpose(out=psT[b][0:64, 0, :], in_=qr[b][:, 0:64], identity=ident[:, :]),
                         reads=[("qr", b), ("qr", b, "hi"), "ident"], writes=[("psT", b)])
                    S.op("act", lambda e, b=b: e.activation(out=trs[b][0:64, 0, :], in_=psT[b][0:64, 0, :], func=AF.Copy),
                         reads=[("psT", b)], writes=[("trs", b)])
                    S.dma("sp", ikT[:, tok], trs[b][0:64, 0, :], reads=[("trs", b)], writes=[("dram", "ik", tt)], key=("st", tt % 4))
        S.emit()
    return nc


def rope_tables(pos, dim):
    inv = np.power(np.float32(10000.0), -np.arange(0, dim, 2, dtype=np.float32) / dim).astype(np.float32)
    ang = pos.astype(np.float32)[:, None] * inv[None, :]
    return np.stack([np.cos(ang), np.sin(ang)], axis=1).astype(np.float32)


def b0_inputs(hc, pos, inp):
    inp = {k: np.asarray(v) for k, v in inp.items()}
    import ml_dtypes
    return {
        "hT": np.ascontiguousarray(hc.T),
        "gn": np.ascontiguousarray(inp["mix_norm"][1].reshape(NDC, 128).T),
        "w": np.ascontiguousarray(inp["odd_w_in"][0]),
        "qnb": np.ascontiguousarray(np.tile(inp["odd_q_norm"][0][None, :], (128, 4))),
        "knb": np.ascontiguousarray(np.tile(inp["odd_k_norm"][0][None, :], (128, 4))),
        "lng": np.ascontiguousarray(np.tile(inp["odd_idx_ln_g"][0][None, :], (128, 1))),
        "lnb": np.ascontiguousarray(np.tile(inp["odd_idx_ln_b"][0][None, :], (128, 1))),
        "cs": rope_tables(pos, 128),
        "cs32": rope_tables(pos, 32),
        "ident": np.eye(128, dtype=np.float32).astype(ml_dtypes.bfloat16),
    }


NIT = 24
SEQ = 8192
NEGM = -30000.0
BIGN = -1.0e30


def fold_tiles(j):
    out = []
    for m in range(8):
        out += [8 * m + j, 8 * m + 7 - j]
    return out


def build_b(npos=16):
    nc = bass.Bass("TRN2", target_bir_lowering=False)
    qTq = nc.dram_tensor("qTq", [16, 128, 16, 128], BF16, kind="ExternalInput").ap()
    iqTq = nc.dram_tensor("iqTq", [16, 64, 16, 128], BF16, kind="ExternalInput").ap()
    iwq_d = nc.dram_tensor("iwq", [128, 16, 16], F32, kind="ExternalInput").ap()
    posq_d = nc.dram_tensor("posq", [128, 16], F32, kind="ExternalInput").ap()
    kTg = nc.dram_tensor("kTg", [4, 128, SEQ], BF16, kind="ExternalInput").ap()
    ikT_d = nc.dram_tensor("ikT", [64, SEQ], BF16, kind="ExternalInput").ap()
    vx = nc.dram_tensor("vx", [4, 128, 64, 129], BF16, kind="ExternalInput").ap()
    ident_d = nc.dram_tensor("ident", [128, 128], BF16, kind="ExternalInput").ap()
    i4_d = nc.dram_tensor("i4", [128, 512], BF16, kind="ExternalInput").ap()
    poskb_d = nc.dram_tensor("poskb", [128, 512], F32, kind="ExternalInput").ap()
    obq = nc.dram_tensor("obq", [16, 128, 16, 128], BF16, kind="ExternalOutput").ap()
    thr_o = nc.dram_tensor("thr", [128, 16, 4], F32, kind="ExternalOutput").ap()
    with ExitStack() as es:
        S = Sched(nc, es)
        ident = S.sb([128, 128], BF16, "ident_sb")
        i4 = S.sb([128, 512], BF16, "i4_sb")
        poskb = S.sb([128, 512], F32, "poskb_sb")
        iwq = S.sb([128, 16, 16], F32, "iwq_sb")
        posq = S.sb([128, 16], F32, "posq_sb")
        ikT = S.sb([64, SEQ], BF16, "ikT_sb")
        sc = S.sb([128, SEQ], F32, "sc")
        mb = S.sb([128, SEQ], BF16, "mb")
        junk = S.sb([128, SEQ], BF16, "junkb")
        qt = [S.sb([128, 16, 128], BF16, f"qt{i}") for i in range(2)]
        iqt = [S.sb([64, 16, 128], BF16, f"iqt{i}") for i in range(2)]
        rb = [S.sb([128, 512], F32, f"rb{i}") for i in range(3)]
        tmpm = S.sb([128, 512], F32, "tmpm")
        kc = [S.sb([128, 512], BF16, f"kc{i}") for i in range(3)]
        vc = [S.sb([128, 4, 129], BF16, f"vc{i}") for i in range(3)]
        pT = [S.sb([128, 512], BF16, f"pT{i}") for i in range(2)]
        obt = S.sb([128, 16, 128], BF16, "obt")
        obT = [S.sb([128, 8, 128], BF16, f"obT{i}") for i in range(2)]
        sm = S.sb([128, 16], F32, "sm")
        thr_sb = S.sb([128, 16, 4], F32, "thr_sb")
        psI = [S.ps([128, 512], F32, f"psI{i}") for i in range(2)]
        psL = [S.ps([128, 512], F32, f"psL{i}") for i in range(2)]
        psO = [S.ps([128, 512], F32, f"psO{i}") for i in range(2)]
        psT = S.ps([128, 8, 128], BF16, "psT")

        for (sbt, dt_, k) in ((ident, ident_d, "ident"), (i4, i4_d, "i4"), (poskb, poskb_d, "poskb"), (posq, posq_d, "posq"),
                              (ikT, ikT_d, "ikT")):
            S.dma("sp", sbt[:], dt_[:, :], writes=[k])
        S.dma("sp", iwq[:], iwq_d[:, :, :], writes=["iwq"])
        S.op("pool", lambda e: e.memset(sm[:, 7:8], 0.5), writes=["half"])
        S.op("pool", lambda e: e.memset(thr_sb[:], 0.0), writes=["thr_sb"])
        cnts = [0, 0, 0, 0, 0]
        for i in range(npos):
            m, e_ = i // 2, i % 2
            nk = 8 * m + 4 + 4 * e_
            NKB = nk // 4
            W = nk * 128
            qb = i % 2
            S.dma("sp", qt[qb][:], qTq[i, :, :, :], writes=[("qt", qb)])
            S.dma("sp", iqt[qb][:], iqTq[i, :, :, :], writes=[("iqt", qb)])
            for h in range(NIH):
                for kb in range(NKB):
                    pb = cnts[0] % 2
                    cnts[0] += 1
                    r3 = cnts[1] % 3
                    cnts[1] += 1
                    S.op("pe", lambda e, pb=pb, h=h, kb=kb, qb=qb: e.matmul(psI[pb][:, :], lhsT=iqt[qb][:, h, :],
                                                                        rhs=ikT[:, kb * 512:(kb + 1) * 512], start=True, stop=True),
                         reads=[("iqt", qb), "ikT"], writes=[("psI", pb)])
                    S.op("act", lambda e, pb=pb, r3=r3: e.activation(out=rb[r3][:, :], in_=psI[pb][:, :], func=AF.Relu),
                         reads=[("psI", pb)], writes=[("rb", r3)])
                    blk = slice(kb * 512, (kb + 1) * 512)
                    if h == 0:
                        S.op("dve", lambda e, r3=r3, blk=blk, i=i, h=h: e.tensor_scalar(
                            out=sc[:, blk], in0=rb[r3][:, :], scalar1=iwq[:, i, h:h + 1], scalar2=None, op0=ALU.mult),
                            reads=[("rb", r3), "iwq"], writes=[("sc", kb)])
                    else:
                        S.op("dve", lambda e, r3=r3, blk=blk, i=i, h=h: e.scalar_tensor_tensor(
                            out=sc[:, blk], in0=rb[r3][:, :], scalar=iwq[:, i, h:h + 1], in1=sc[:, blk], op0=ALU.mult, op1=ALU.add),
                            reads=[("rb", r3), "iwq", ("sc", kb)], writes=[("sc", kb)])
            sck = [("sc", kb) for kb in range(NKB)]
            S.op("dve", lambda e, W=W: e.reduce_max(out=sm[:, 0:1], in_=sc[:, :W], axis=AX.X, apply_absolute_value=True),
                 reads=sck, writes=["amax"])
            lastb = slice(W - 512, W)
            S.op("dve", lambda e, i=i, W=W: e.tensor_scalar(out=sm[:, 8:9], in0=posq[:, i:i + 1], scalar1=float(-(W - 512)), scalar2=None,
                                                            op0=ALU.add), reads=["posq"], writes=["pqadj"])
            S.op("dve", lambda e: e.tensor_scalar(out=tmpm[:, :], in0=poskb[:, :], scalar1=sm[:, 8:9], scalar2=BIGN,
                                                  op0=ALU.is_gt, op1=ALU.mult), reads=["poskb", "pqadj"], writes=["tmpm"])
            S.op("dve", lambda e, lastb=lastb: e.tensor_tensor(out=sc[:, lastb], in0=sc[:, lastb], in1=tmpm[:, :], op=ALU.add),
                 reads=["tmpm", "amax"] + sck, writes=[("sc", NKB - 1)])
            S.op("dve", lambda e: e.tensor_scalar(out=sm[:, 1:2], in0=sm[:, 0:1], scalar1=-1.0, scalar2=-1.0, op0=ALU.mult, op1=ALU.add),
                 reads=["amax"], writes=["lo"])
            S.op("dve", lambda e: e.tensor_scalar(out=sm[:, 2:3], in0=sm[:, 0:1], scalar1=1.0, scalar2=None, op0=ALU.add),
                 reads=["amax"], writes=["hi"])
            for it in range(NIT):
                S.op("dve", lambda e: e.scalar_tensor_tensor(out=sm[:, 3:4], in0=sm[:, 1:2], scalar=sm[:, 2:3], in1=sm[:, 7:8],
                                                             op0=ALU.add, op1=ALU.mult), reads=["lo", "hi", "half"], writes=["mid"])
                S.op("dve", lambda e, W=W: e.tensor_scalar(out=junk[:, :W], in0=sc[:, :W], scalar1=sm[:, 3:4], scalar2=0.0,
                                                           op0=ALU.is_gt, op1=ALU.add, accum_out=sm[:, 4:5]),
                     reads=["mid"] + sck, writes=["junk", "cnt"])
                S.op("dve", lambda e: e.tensor_scalar(out=sm[:, 5:6], in0=sm[:, 4:5], scalar1=256.5, scalar2=None, op0=ALU.is_gt),
                     reads=["cnt"], writes=["pred"])
                S.op("dve", lambda e: e.tensor_tensor(out=sm[:, 6:7], in0=sm[:, 3:4], in1=sm[:, 1:2], op=ALU.subtract),
                     reads=["mid", "lo"], writes=["d"])
                S.op("dve", lambda e: e.scalar_tensor_tensor(out=sm[:, 1:2], in0=sm[:, 6:7], scalar=sm[:, 5:6], in1=sm[:, 1:2],
                                                             op0=ALU.mult, op1=ALU.add), reads=["d", "pred", "lo"], writes=["lo"])
                S.op("dve", lambda e: e.tensor_tensor(out=sm[:, 6:7], in0=sm[:, 2:3], in1=sm[:, 3:4], op=ALU.subtract),
                     reads=["mid", "hi"], writes=["d"])
                S.op("dve", lambda e: e.scalar_tensor_tensor(out=sm[:, 2:3], in0=sm[:, 6:7], scalar=sm[:, 5:6], in1=sm[:, 3:4],
                                                             op0=ALU.mult, op1=ALU.add), reads=["d", "pred", "mid"], writes=["hi"])
            S.op("dve", lambda e, W=W: e.tensor_scalar(out=mb[:, :W], in0=sc[:, :W], scalar1=sm[:, 2:3], scalar2=NEGM,
                                                       op0=ALU.is_le, op1=ALU.mult), reads=["hi"] + sck, writes=["mb"])
            S.op("dve", lambda e, i=i: e.tensor_copy(out=thr_sb[:, i, 0:3], in_=sm[:, 0:3]), reads=["amax", "lo", "hi", "thr_sb"], writes=["thr_sb"])
            S.op("dve", lambda e, i=i: e.tensor_copy(out=thr_sb[:, i, 3:4], in_=sm[:, 4:5]), reads=["cnt", "thr_sb"], writes=["thr_sb"])
            for g in range(NKV):
                for c4 in range(NKB):
                    s3 = cnts[2] % 3
                    cnts[2] += 1
                    S.dma("sp", kc[s3][:, :], kTg[g, :, c4 * 512:(c4 + 1) * 512], writes=[("kc", s3)])
                    S.dma("sp", vc[s3][:, :, :], vx[g, :, c4 * 4:(c4 + 1) * 4, :], writes=[("vc", s3)])
                    for s4 in range(4):
                        stl = c4 * 4 + s4
                        lb = cnts[3] % 2
                        cnts[3] += 1
                        S.op("pe", lambda e, lb=lb, s3=s3, s4=s4, g=g, qb=qb: e.matmul(
                            psL[lb][:, :], lhsT=kc[s3][:, s4 * 128:(s4 + 1) * 128],
                            rhs=qt[qb][:, g * 4:(g + 1) * 4, :].rearrange("p h t -> p (h t)"), start=True, stop=False),
                            reads=[("kc", s3), ("qt", qb)], writes=[("psL", lb)])
                        S.op("pe", lambda e, lb=lb, stl=stl: e.matmul(
                            psL[lb][:, :], lhsT=mb[:, stl * 128:(stl + 1) * 128], rhs=i4[:, :], start=False, stop=True),
                            reads=["mb", "i4"], writes=[("psL", lb)])
                        S.op("act", lambda e, lb=lb: e.activation(out=pT[lb][:, :], in_=psL[lb][:, :], func=AF.Exp, scale=float(HD ** -0.5)),
                             reads=[("psL", lb)], writes=[("pT", lb)])
                        for h4 in range(4):
                            S.op("pe", lambda e, lb=lb, h4=h4, s3=s3, s4=s4, stl=stl, nk=nk: e.matmul(
                                psO[h4 // 2][:, (h4 % 2) * 256:(h4 % 2) * 256 + 129], lhsT=pT[lb][:, h4 * 128:(h4 + 1) * 128],
                                rhs=vc[s3][:, s4, :], start=(stl == 0 and h4 % 2 == 0), stop=(stl == nk - 1),
                                skip_group_check=True),
                                reads=[("pT", lb), ("vc", s3)], writes=[("psO", h4 // 2)])
                for h4 in range(4):
                    reg = psO[h4 // 2][:, (h4 % 2) * 256:(h4 % 2) * 256 + 129]
                    S.op("dve", lambda e, reg=reg, h4=h4: e.reciprocal(out=sm[:, 9 + h4:10 + h4], in_=reg[:, 128:129]),
                         reads=[], writes=[("rec", h4), ("psO", h4 // 2)])
                    S.op("act", lambda e, reg=reg, h4=h4, g=g: e.activation(out=obt[:, g * 4 + h4, :], in_=reg[:, 0:128], func=AF.Copy,
                                                                         scale=sm[:, 9 + h4:10 + h4]),
                         reads=[("rec", h4)], writes=[("obt", g * 4 + h4), ("psO", h4 // 2)])
            for half in range(2):
                for h8 in range(8):
                    h = half * 8 + h8
                    S.op("pe", lambda e, h=h, h8=h8: e.transpose(out=psT[:, h8, :], in_=obt[:, h, :], identity=ident[:, :]),
                         reads=[("obt", h), "ident"], writes=["psT"])
                ob_ = cnts[4] % 2
                cnts[4] += 1
                S.op("dve", lambda e, ob_=ob_: e.tensor_copy(out=obT[ob_][:, :, :], in_=psT[:, :, :]), reads=["psT"], writes=[("obT", ob_)])
                S.dma("sp", obq[i, :, half * 8:(half + 1) * 8, :], obT[ob_][:, :, :], reads=[("obT", ob_)], writes=[("dram", "obq", i, half)],
                      key=("st", ob_))
        S.dma("sp", thr_o[:, :, :], thr_sb[:, :, :], reads=["thr_sb"], writes=[("dram", "thr")])
        S.emit()
    return nc


def b_inputs(proj, j, tiles):
    import ml_dtypes
    bf = ml_dtypes.bfloat16
    qTq = np.stack([proj["qT"][:, :, t * 128:(t + 1) * 128].transpose(1, 0, 2) for t in tiles])
    iqTq = np.stack([proj["iqT"][:, :, t * 128:(t + 1) * 128].transpose(1, 0, 2) for t in tiles])
    iwq = np.stack([proj["iw"][t * 128:(t + 1) * 128] for t in tiles], axis=1)
    posq = np.stack([np.arange(t * 128, (t + 1) * 128, dtype=np.float32) for t in tiles], axis=1)
    v = proj["v"].reshape(64, 128, 4, 128)
    vx = np.concatenate([v, np.ones((64, 128, 4, 1), dtype=v.dtype)], axis=-1).transpose(2, 1, 0, 3)
    return {
        "qTq": np.ascontiguousarray(qTq), "iqTq": np.ascontiguousarray(iqTq), "iwq": np.ascontiguousarray(iwq.astype(np.float32)),
        "posq": np.ascontiguousarray(posq), "kTg": np.ascontiguousarray(proj["kT"]), "ikT": np.ascontiguousarray(proj["ikT"]),
        "vx": np.ascontiguousarray(vx), "ident": np.eye(128, dtype=np.float32).astype(bf),
        "i4": np.tile(np.eye(128, dtype=np.float32), (1, 4)).astype(bf),
        "poskb": np.tile(np.arange(512, dtype=np.float32)[None, :], (128, 1)),
    }


EVEN_COLS = 6152
GC1 = 1.5957691216057308
GC2 = 0.044715


def build_a(state_only=False):
    nc = bass.Bass("TRN2", target_bir_lowering=False)
    xT = nc.dram_tensor("xT", [D, T], F32, kind="ExternalInput").ap()
    gn_d = nc.dram_tensor("gn", [128, NDC], F32, kind="ExternalInput").ap()
    w = nc.dram_tensor("w", [D, 2112 if state_only else EVEN_COLS], F32, kind="ExternalInput").ap()
    gb_d = nc.dram_tensor("gb", [128, 8], F32, kind="ExternalInput").ap()
    tri_d = nc.dram_tensor("tri", [128, 128], F32, kind="ExternalInput").ap()
    if not state_only:
        anb_d = nc.dram_tensor("anb", [128, 1024], F32, kind="ExternalInput").ap()
        bnb_d = nc.dram_tensor("bnb", [128, 1024], F32, kind="ExternalInput").ap()
        spT_d = nc.dram_tensor("spT", [128, 4, 128], F32, kind="ExternalInput").ap()
        sbias_d = nc.dram_tensor("sbias", [128, 4], F32, kind="ExternalInput").ap()
        ident_d = nc.dram_tensor("ident", [128, 128], BF16, kind="ExternalInput").ap()
        Sall = nc.dram_tensor("Sall", [4, 8, 128, 257], F32, kind="ExternalInput").ap()
        Ball_d = nc.dram_tensor("Ball", [128, 16], F32, kind="ExternalInput").ap()
        M_d = nc.dram_tensor("Msel", [128, 4], F32, kind="ExternalInput").ap()
        Tm_d = nc.dram_tensor("Tsel", [128, 16], F32, kind="ExternalInput").ap()
        mT = nc.dram_tensor("mT", [D, T], BF16, kind="ExternalOutput").ap()
        qTs = nc.dram_tensor("qTs", [1024, T], BF16).ap()
        kTs = nc.dram_tensor("kTs", [1024, T], BF16).ap()
        oS = nc.dram_tensor("oS", [T, 1024], BF16).ap()
        uS = nc.dram_tensor("uS", [T, 1024], BF16).ap()
        zS = nc.dram_tensor("zS", [T, 1024], BF16).ap()
    else:
        Sout = nc.dram_tensor("Sout", [8, 128, 257], F32, kind="ExternalOutput").ap()
        Bout = nc.dram_tensor("Bout", [128, 4], F32, kind="ExternalOutput").ap()
    kS = nc.dram_tensor("kS", [T, 1024], BF16).ap()
    vS = nc.dram_tensor("vS", [T, 1024], BF16).ap()
    NT = T // 128
    with ExitStack() as es:
        S = Sched(nc, es)
        ones = S.sb([128, 128], BF16, "ones")
        onesf = S.sb([128, 128], F32, "onesf")
        tri = S.sb([128, 128], F32, "tri_sb")
        gn = S.sb([128, NDC], F32, "gn_sb")
        gb = S.sb([128, 8], F32, "gb_sb")
        xn = S.sb([128, NDC, T], BF16, "xn")
        hsub = S.sb([128, NDC, 256], F32, "hsub")
        sq = S.sb([128, NDC, 256], BF16, "sq")
        rstd = S.sb([128, 256], F32, "rstd")
        wt = [S.sb([128, NDC, 512], BF16, f"wt{i}") for i in range(2)]
        gat = S.sb([128, NT, 8], F32, "gat")
        xs = [S.sb([128, 512], F32, f"xs{i}") for i in range(2)]
        x2 = [S.sb([128, 512], F32, f"x2{i}") for i in range(2)]
        ob = [S.sb([128, 512], BF16, f"ob{i}") for i in range(3)]
        st = [S.sb([128, 8], F32, f"st{i}") for i in range(2)]
        junk = S.sb([128, 256], F32, "junk")
        banks = [S.ps([128, 512], F32, f"bank{i}") for i in range(8)]
        BK = lambda i: ("bank", i)

        S.op("pool", lambda e: e.memset(ones[:], 1.0), writes=["ones"])
        S.op("pool", lambda e: e.memset(onesf[:], 1.0), writes=["onesf"])
        S.dma("sp", gn[:], gn_d[:, :], writes=["gn"])
        S.dma("sp", gb[:], gb_d[:, :], writes=["gb"])
        S.dma("sp", tri[:], tri_d[:, :], writes=["tri"])
        w_v = w.rearrange("(c p) n -> p c n", p=128)
        for i in range(T // 256):
            rms_stage(S, xT, i * 256, 256, hsub, sq, banks[7], rstd, ones, gn, xn, i * 256, "xT")
        n = [0, 0, 0, 0]
        if not state_only:
            for (col0, dst, scale) in ((0, qTs, 1.0 / 16), (1024, kTs, 1.0)):
                for half in range(2):
                    s = n[0] % 2
                    n[0] += 1
                    S.dma("pool", wt[s][:, :, :], w_v[:, :, col0 + half * 512:col0 + (half + 1) * 512], reads=[("dram", "w")], writes=[("wt", s)])
                    for cc in range(4):
                        for tg in range(T // 512):
                            pb = n[1] % 3
                            n[1] += 1
                            for c in range(NDC):
                                S.op("pe", lambda e, pb=pb, c=c, cc=cc, tg=tg, s=s: e.matmul(
                                    banks[pb][:, :], lhsT=wt[s][:, c, cc * 128:(cc + 1) * 128], rhs=xn[:, c, tg * 512:(tg + 1) * 512],
                                    start=(c == 0), stop=(c == NDC - 1)), reads=[("wt", s), ("xn", c)], writes=[BK(pb)])
                            b3 = n[2] % 3
                            n[2] += 1
                            S.op("act", lambda e, pb=pb, b3=b3, scale=scale: e.activation(out=ob[b3][:, :], in_=banks[pb][:, :], func=AF.Copy, scale=scale),
                                 reads=[], writes=[BK(pb), ("ob", b3)])
                            r0 = (half * 4 + cc) * 128
                            S.dma("sp", dst[r0:r0 + 128, tg * 512:(tg + 1) * 512], ob[b3][:, :], reads=[("ob", b3)],
                                  writes=[("dram", "fm", col0, half * 4 + cc, tg)], key=("st", n[2] % 4))
        tiles = [(1024 + i * 512, 512, "k", i) for i in range(2)] + [(2048 + i * 512, 512, "v", i) for i in range(2)] + [(4096, 64, "g", 0)]
        if state_only:
            tiles = [(i * 512, 512, "k", i) for i in range(2)] + [(1024 + i * 512, 512, "v", i) for i in range(2)] + [(2048, 64, "g", 0)]
        if not state_only:
            tiles += [(3072 + i * 512, 512, "o", i) for i in range(2)] + [(4104 + i * 512, 512, "u", i) for i in range(2)] + \
                     [(5128 + i * 512, 512, "z", i) for i in range(2)]
            bnb = S.sb([128, 1024], F32, "bnb_sb")
            S.dma("sp", bnb[:], bnb_d[:, :], writes=["bnb"])
        for (col0, cw_, kind, idx) in tiles:
            s = n[0] % 2
            n[0] += 1
            S.dma("pool", wt[s][:, :, :cw_], w_v[:, :, col0:col0 + cw_], reads=[("dram", "w")], writes=[("wt", s)])
            for tt in range(NT):
                pb = n[1] % 3
                n[1] += 1
                ps = banks[pb]
                for c in range(NDC):
                    S.op("pe", lambda e, ps=ps, c=c, tt=tt, s=s, cw_=cw_: e.matmul(
                        ps[:, :cw_], lhsT=xn[:, c, tt * 128:(tt + 1) * 128], rhs=wt[s][:, c, :cw_],
                        start=(c == 0), stop=(c == NDC - 1)), reads=[("wt", s), ("xn", c)], writes=[BK(pb)])
                tok = slice(tt * 128, (tt + 1) * 128)
                b3 = n[2] % 3
                n[2] += 1
                cs_ = slice(idx * 512, (idx + 1) * 512)
                if kind in ("k", "v"):
                    S.op("act", lambda e, ps=ps, b3=b3: e.activation(out=ob[b3][:, :], in_=ps[:, :], func=AF.Copy),
                         reads=[], writes=[BK(pb), ("ob", b3)])
                    S.dma("sp", (kS if kind == "k" else vS)[tok, cs_], ob[b3][:, :], reads=[("ob", b3)],
                          writes=[("dram", kind, idx, tt)], key=("st", n[2] % 4))
                elif kind == "o":
                    S.op("act", lambda e, ps=ps, b3=b3: e.activation(out=ob[b3][:, :], in_=ps[:, :], func=AF.Sigmoid),
                         reads=[], writes=[BK(pb), ("ob", b3)])
                    S.dma("sp", oS[tok, cs_], ob[b3][:, :], reads=[("ob", b3)], writes=[("dram", kind, idx, tt)], key=("st", n[2] % 4))
                elif kind == "g":
                    S.op("dve", lambda e, ps=ps, tt=tt: e.tensor_tensor(out=gat[:, tt, :], in0=ps[:, 0:8], in1=gb[:, :], op=ALU.add),
                         reads=["gb"], writes=[BK(pb), ("gat", tt)])
                    S.op("act", lambda e, tt=tt: e.activation(out=gat[:, tt, 4:8], in_=gat[:, tt, 4:8], func=AF.Exp, scale=-1.0),
                         reads=[("gat", tt)], writes=[("gat", tt)])
                    S.op("dve", lambda e, tt=tt: e.tensor_scalar(out=gat[:, tt, 4:8], in0=gat[:, tt, 4:8], scalar1=1.0, scalar2=None, op0=ALU.add),
                         reads=[("gat", tt)], writes=[("gat", tt)])
                    S.op("act", lambda e, tt=tt: e.activation(out=gat[:, tt, 4:8], in_=gat[:, tt, 4:8], func=AF.Ln),
                         reads=[("gat", tt)], writes=[("gat", tt)])
                    S.op("dve", lambda e, tt=tt: e.tensor_scalar(out=gat[:, tt, 4:8], in0=gat[:, tt, 4:8], scalar1=-1.0, scalar2=None, op0=ALU.mult),
                         reads=[("gat", tt)], writes=[("gat", tt)])
                else:
                    xb = n[3] % 2
                    n[3] += 1
                    S.op("act", lambda e, ps=ps, xb=xb: e.activation(out=xs[xb][:, :], in_=ps[:, :], func=AF.Copy),
                         reads=[], writes=[BK(pb), ("xs", xb)])
                    S.op("pool", lambda e, xb=xb: e.tensor_tensor(out=x2[xb][:, :], in0=xs[xb][:, :], in1=xs[xb][:, :], op=ALU.mult),
                         reads=[("xs", xb)], writes=[("x2", xb)])
                    S.op("pool", lambda e, xb=xb: e.tensor_scalar(out=x2[xb][:, :], in0=x2[xb][:, :], scalar1=GC2, scalar2=1.0, op0=ALU.mult, op1=ALU.add),
                         reads=[("x2", xb)], writes=[("x2", xb)])
                    S.op("dve", lambda e, xb=xb: e.tensor_tensor(out=x2[xb][:, :], in0=x2[xb][:, :], in1=xs[xb][:, :], op=ALU.mult),
                         reads=[("x2", xb), ("xs", xb)], writes=[("x2", xb)])
                    S.op("act", lambda e, xb=xb: e.activation(out=x2[xb][:, :], in_=x2[xb][:, :], func=AF.Sigmoid, scale=GC1),
                         reads=[("x2", xb)], writes=[("x2", xb)])
                    if kind == "u":
                        S.op("dve", lambda e, xb=xb, b3=b3: e.tensor_tensor(out=ob[b3][:, :], in0=x2[xb][:, :], in1=xs[xb][:, :], op=ALU.mult),
                             reads=[("x2", xb), ("xs", xb)], writes=[("ob", b3)])
                        S.dma("sp", uS[tok, cs_], ob[b3][:, :], reads=[("ob", b3)], writes=[("dram", kind, idx, tt)], key=("st", n[2] % 4))
                    else:
                        S.op("dve", lambda e, xb=xb: e.tensor_tensor(out=xs[xb][:, :], in0=x2[xb][:, :], in1=xs[xb][:, :], op=ALU.mult),
                             reads=[("x2", xb), ("xs", xb)], writes=[("xs", xb)])
                        sb_ = n[3] % 2
                        S.op("pool", lambda e, sb_=sb_: e.memset(st[sb_][:], 0.0), writes=[("st", sb_)])
                        for gg in range(2):
                            S.op("act", lambda e, xb=xb, gg=gg, sb_=sb_: e.activation(out=junk[:, :], in_=xs[xb][:, gg * 256:(gg + 1) * 256], func=AF.Square,
                                                                               accum_out=st[sb_][:, gg:gg + 1]),
                                 reads=[("xs", xb), ("st", sb_)], writes=["junk", ("st", sb_, gg)])
                        S.op("dve", lambda e, sb_=sb_: e.tensor_scalar(out=st[sb_][:, 2:4], in0=st[sb_][:, 0:2], scalar1=1.0 / 256, scalar2=EPS,
                                                                     op0=ALU.mult, op1=ALU.add), reads=[("st", sb_, 0), ("st", sb_, 1)], writes=[("st2", sb_)])
                        S.op("act", lambda e, sb_=sb_: e.activation(out=st[sb_][:, 2:4], in_=st[sb_][:, 2:4], func=AF.Sqrt), reads=[("st2", sb_)], writes=[("st2", sb_)])
                        S.op("dve", lambda e, sb_=sb_: e.reciprocal(out=st[sb_][:, 2:4], in_=st[sb_][:, 2:4]), reads=[("st2", sb_)], writes=[("st2", sb_)])
                        for gg in range(2):
                            S.op("dve", lambda e, xb=xb, gg=gg, sb_=sb_, b3=b3, idx=idx: e.scalar_tensor_tensor(
                                out=ob[b3][:, gg * 256:(gg + 1) * 256], in0=xs[xb][:, gg * 256:(gg + 1) * 256], scalar=st[sb_][:, 2 + gg:3 + gg],
                                in1=bnb[:, idx * 512 + gg * 256:idx * 512 + (gg + 1) * 256], op0=ALU.mult, op1=ALU.mult),
                                reads=[("xs", xb), ("st2", sb_), "bnb"], writes=[("ob", b3)])
                        S.dma("sp", zS[tok, cs_], ob[b3][:, :], reads=[("ob", b3)], writes=[("dram", kind, idx, tt)], key=("st", n[2] % 4))
        C = [S.sb([128, 2, 257], F32, f"C{h}") for h in range(4)]
        Cb = [S.sb([128, 2, 257], BF16, f"Cb{h}") for h in range(4)]
        sm = S.sb([128, 64], F32, "sm")
        kt = [S.sb([128, 1024], BF16, f"kt{i}") for i in range(2)]
        vt = [S.sb([128, 1024], BF16, f"vt{i}") for i in range(2)]
        vp = [S.sb([128, 257], BF16, f"vp{i}") for i in range(2)]
        ctmp = S.sb([128, 257], F32, "ctmp")
        fm_keys = []
        if not state_only:
            anb = S.sb([128, 1024], F32, "anb_sb")
            spT = S.sb([128, 4, 128], F32, "spT_sb")
            spTm = S.sb([128, 4, 128], BF16, "spTm")
            sbias = S.sb([128, 4], F32, "sbias_sb")
            ident = S.sb([128, 128], BF16, "ident_sb")
            Ball = S.sb([128, 16], F32, "Ball_sb")
            Msel = S.sb([128, 4], F32, "Msel_sb")
            Tsel = S.sb([128, 16], F32, "Tsel_sb")
            qTt = [S.sb([128, 8, 128], BF16, f"qTt{i}") for i in range(2)]
            kTt = [S.sb([128, 8, 128], BF16, f"kTt{i}") for i in range(2)]
            ot = [S.sb([128, 1024], BF16, f"ot{i}") for i in range(2)]
            ut = [S.sb([128, 1024], BF16, f"ut{i}") for i in range(2)]
            zt = [S.sb([128, 1024], BF16, f"zt{i}") for i in range(2)]
            sT = [S.sb([128, 128], BF16, f"sT{i}") for i in range(2)]
            htmp = S.sb([128, 256], F32, "htmp")
            mtok = [S.sb([128, 2048], BF16, f"mtok{i}") for i in range(2)]
            mTs = [S.sb([128, 8, 128], BF16, f"mTs{i}") for i in range(2)]
            sin = S.sb([128, 257], F32, "sin")
            for (sbt, dt_, k) in ((anb, anb_d, "anb"), (sbias, sbias_d, "sbias"), (ident, ident_d, "ident"), (Ball, Ball_d, "Ball"),
                                  (Msel, M_d, "Msel"), (Tsel, Tm_d, "Tsel")):
                S.dma("sp", sbt[:], dt_[:, :], writes=[k])
            S.dma("sp", spT[:], spT_d[:, :, :], writes=["spT"])
            for g in range(4):
                S.op("pool", lambda e, g=g: e.tensor_tensor(out=spTm[:, g, :], in0=spT[:, g, :], in1=tri[:, :], op=ALU.mult),
                     reads=["spT", "tri"], writes=[("spTm", g)])
            for i in range(4):
                for l in range(4):
                    if l == 0:
                        S.op("dve", lambda e, i=i, l=l: e.tensor_scalar(out=sm[:, 32 + 4 * i:36 + 4 * i], in0=Ball[:, 4 * l:4 * l + 4],
                                                                      scalar1=Tsel[:, 4 * i + l:4 * i + l + 1], scalar2=None, op0=ALU.mult),
                             reads=["Ball", "Tsel"], writes=[("coef", i)])
                    else:
                        S.op("dve", lambda e, i=i, l=l: e.scalar_tensor_tensor(out=sm[:, 32 + 4 * i:36 + 4 * i], in0=Ball[:, 4 * l:4 * l + 4],
                                                                             scalar=Tsel[:, 4 * i + l:4 * i + l + 1], in1=sm[:, 32 + 4 * i:36 + 4 * i],
                                                                             op0=ALU.mult, op1=ALU.add),
                             reads=["Ball", "Tsel", ("coef", i)], writes=[("coef", i)])
                S.op("act", lambda e, i=i: e.activation(out=sm[:, 32 + 4 * i:36 + 4 * i], in_=sm[:, 32 + 4 * i:36 + 4 * i], func=AF.Exp),
                     reads=[("coef", i)], writes=[("coef", i)])
                S.op("dve", lambda e, i=i: e.tensor_scalar(out=sm[:, 32 + 4 * i:36 + 4 * i], in0=sm[:, 32 + 4 * i:36 + 4 * i],
                                                           scalar1=Msel[:, i:i + 1], scalar2=None, op0=ALU.mult),
                     reads=[("coef", i), "Msel"], writes=[("coef", i)])
            for h in range(4):
                for dc in range(2):
                    for i in range(4):
                        S.dma("sp", sin[:, :], Sall[i, h * 2 + dc, :, :], writes=["sin"])
                        if i == 0:
                            S.op("dve", lambda e, h=h, dc=dc, i=i: e.tensor_scalar(out=C[h][:, dc, :], in0=sin[:, :], scalar1=sm[:, 32 + 4 * i + h:33 + 4 * i + h],
                                                                                 scalar2=None, op0=ALU.mult), reads=["sin", ("coef", i)], writes=[("C", h, dc)])
                        else:
                            S.op("dve", lambda e, h=h, dc=dc, i=i: e.scalar_tensor_tensor(out=C[h][:, dc, :], in0=sin[:, :], scalar=sm[:, 32 + 4 * i + h:33 + 4 * i + h],
                                                                                        in1=C[h][:, dc, :], op0=ALU.mult, op1=ALU.add),
                                 reads=["sin", ("coef", i), ("C", h, dc)], writes=[("C", h, dc)])
                    S.op("pool", lambda e, h=h, dc=dc: e.tensor_copy(out=Cb[h][:, dc, :], in_=C[h][:, dc, :]), reads=[("C", h, dc)], writes=[("Cb", h, dc)])
            fm_keys = [("dram", "fm", col0, r, tg) for col0 in (0, 1024) for r in range(8) for tg in range(T // 512)]
        else:
            for h in range(4):
                S.op("pool", lambda e, h=h: e.memset(C[h][:], 0.0), writes=[("C", h, 0), ("C", h, 1)])
            S.op("pool", lambda e: e.memset(sm[:, 48:52], 0.0), writes=["btot"])
        kv_keys = lambda kind, tt: [("dram", kind, idx, tt) for idx in range(2)]
        for tt in range(NT):
            b2 = tt % 2
            tok = slice(tt * 128, (tt + 1) * 128)
            S.dma("sp", kt[b2][:, :], kS[tok, :], reads=kv_keys("k", tt), writes=[("kt", b2)])
            S.dma("sp", vt[b2][:, :], vS[tok, :], reads=kv_keys("v", tt), writes=[("vt", b2)])
            if not state_only:
                S.dma("sp", qTt[b2][:, :, :], qTs.rearrange("(c p) t -> p c t", p=128)[:, :, tok], reads=fm_keys, writes=[("qTt", b2)])
                S.dma("sp", kTt[b2][:, :, :], kTs.rearrange("(c p) t -> p c t", p=128)[:, :, tok], reads=fm_keys, writes=[("kTt", b2)])
                S.dma("sp", ot[b2][:, :], oS[tok, :], reads=kv_keys("o", tt), writes=[("ot", b2)])
                S.dma("sp", ut[b2][:, :], uS[tok, :], reads=kv_keys("u", tt), writes=[("ut", b2)])
                S.dma("sp", zt[b2][:, :], zS[tok, :], reads=kv_keys("z", tt), writes=[("zt", b2)])
            S.op("pe", lambda e, tt=tt: e.matmul(banks[0][:, 0:4], lhsT=tri[:, :], rhs=gat[:, tt, 4:8], start=True, stop=True),
                 reads=["tri", ("gat", tt)], writes=[BK(0)])
            S.op("pe", lambda e, tt=tt: e.matmul(banks[0][:, 4:8], lhsT=onesf[:, :], rhs=gat[:, tt, 4:8], start=True, stop=True),
                 reads=["onesf", ("gat", tt)], writes=[BK(0)])
            S.op("dve", lambda e, tt=tt: e.tensor_tensor(out=sm[:, 12:16], in0=gat[:, tt, 0:4], in1=banks[0][:, 0:4], op=ALU.subtract),
                 reads=[("gat", tt)], writes=[BK(0), "d"])
            S.op("act", lambda e: e.activation(out=sm[:, 0:4], in_=sm[:, 12:16], func=AF.Exp), reads=["d"], writes=["w"])
            S.op("act", lambda e: e.activation(out=sm[:, 4:8], in_=banks[0][:, 4:8], func=AF.Exp), reads=[], writes=[BK(0), "eB"])
            if state_only:
                S.op("dve", lambda e: e.tensor_tensor(out=sm[:, 48:52], in0=sm[:, 48:52], in1=banks[0][:, 4:8], op=ALU.add),
                     reads=["btot"], writes=[BK(0), "btot"])
            else:
                S.op("act", lambda e: e.activation(out=sm[:, 8:12], in_=banks[0][:, 0:4], func=AF.Exp, scale=-1.0), reads=[], writes=[BK(0), "emb"])
            mk = b2
            for h in range(4):
                hs = slice(h * 256, (h + 1) * 256)
                vb = h % 2
                S.op("pool", lambda e, vb=vb, b2=b2, hs=hs, h=h: e.tensor_scalar(out=vp[vb][:, 0:256], in0=vt[b2][:, hs], scalar1=sm[:, h:h + 1],
                                                                             scalar2=None, op0=ALU.mult), reads=[("vt", b2), "w"], writes=[("vp", vb)])
                S.op("pool", lambda e, vb=vb, h=h: e.tensor_copy(out=vp[vb][:, 256:257], in_=sm[:, h:h + 1]), reads=["w"], writes=[("vp", vb, 1)])
                vpk = [("vp", vb), ("vp", vb, 1)]
                if not state_only:
                    for dc in range(2):
                        S.op("pe", lambda e, dc=dc, b2=b2, h=h: e.matmul(banks[1][:, 0:128], lhsT=kTt[b2][:, h * 2 + dc, :], rhs=qTt[b2][:, h * 2 + dc, :],
                                                                      start=(dc == 0), stop=(dc == 1)), reads=[("kTt", b2), ("qTt", b2)], writes=[BK(1)])
                    S.op("dve", lambda e, vb=vb: e.tensor_tensor(out=sT[vb][:, :], in0=banks[1][:, 0:128], in1=tri[:, :], op=ALU.mult),
                         reads=["tri"], writes=[BK(1), ("sT", vb)])
                    S.op("pe", lambda e, vb=vb: e.matmul(banks[2][:, 0:257], lhsT=sT[vb][:, :], rhs=vp[vb][:, :], start=True, stop=False),
                         reads=[("sT", vb)] + vpk, writes=[BK(2)])
                    for dc in range(2):
                        S.op("pe", lambda e, dc=dc, b2=b2, h=h: e.matmul(banks[2][:, 0:257], lhsT=qTt[b2][:, h * 2 + dc, :], rhs=Cb[h][:, dc, :],
                                                                      start=False, stop=(dc == 1)), reads=[("qTt", b2), ("Cb", h, dc)], writes=[BK(2)])
                for dc in range(2):
                    S.op("pe", lambda e, dc=dc, b2=b2, h=h, vb=vb: e.matmul(banks[3 + dc][:, 0:257], lhsT=kt[b2][:, h * 256 + dc * 128:h * 256 + (dc + 1) * 128],
                                                                         rhs=vp[vb][:, :], start=True, stop=True), reads=[("kt", b2)] + vpk, writes=[BK(3 + dc)])
                    S.op("dve", lambda e, dc=dc, h=h: e.tensor_tensor(out=ctmp[:, :], in0=banks[3 + dc][:, 0:257], in1=C[h][:, dc, :], op=ALU.add),
                         reads=[("C", h, dc), ("Cb", h, dc)], writes=[BK(3 + dc), "ctmp"])
                    S.op("dve", lambda e, dc=dc, h=h: e.tensor_scalar(out=C[h][:, dc, :], in0=ctmp[:, :], scalar1=sm[:, 4 + h:5 + h], scalar2=None, op0=ALU.mult),
                         reads=["ctmp", "eB"], writes=[("C", h, dc)])
                    if not state_only:
                        S.op("pool", lambda e, dc=dc, h=h: e.tensor_copy(out=Cb[h][:, dc, :], in_=C[h][:, dc, :]), reads=[("C", h, dc)], writes=[("Cb", h, dc)])
                if not state_only:
                    o0 = 16 + 8 * h
                    S.op("act", lambda e, o0=o0, h=h: e.activation(out=sm[:, o0:o0 + 1], in_=banks[2][:, 256:257], func=AF.Abs),
                         reads=[], writes=[BK(2), ("r", h)])
                    S.op("dve", lambda e, o0=o0, h=h: e.tensor_tensor(out=sm[:, o0:o0 + 1], in0=sm[:, o0:o0 + 1], in1=sm[:, 8 + h:9 + h], op=ALU.max),
                         reads=["emb", ("r", h)], writes=[("r", h)])
                    S.op("dve", lambda e, o0=o0: e.reciprocal(out=sm[:, o0:o0 + 1], in_=sm[:, o0:o0 + 1]), reads=[("r", h)], writes=[("r", h)])
                    S.op("pool", lambda e, o0=o0: e.memset(sm[:, o0 + 1:o0 + 2], 0.0), writes=[("ms", h)])
                    S.op("act", lambda e, o0=o0: e.activation(out=junk[:, :], in_=banks[2][:, 0:256], func=AF.Square, accum_out=sm[:, o0 + 1:o0 + 2]),
                         reads=[("ms", h)], writes=[BK(2), "junk", ("ms2", h)])
                    S.op("dve", lambda e, o0=o0: e.tensor_tensor(out=sm[:, o0 + 2:o0 + 3], in0=sm[:, o0:o0 + 1], in1=sm[:, o0:o0 + 1], op=ALU.mult),
                         reads=[("r", h)], writes=[("t", h)])
                    S.op("dve", lambda e, o0=o0: e.tensor_tensor(out=sm[:, o0 + 2:o0 + 3], in0=sm[:, o0 + 2:o0 + 3], in1=sm[:, o0 + 1:o0 + 2], op=ALU.mult),
                         reads=[("t", h), ("ms2", h)], writes=[("t", h)])
                    S.op("dve", lambda e, o0=o0: e.tensor_scalar(out=sm[:, o0 + 2:o0 + 3], in0=sm[:, o0 + 2:o0 + 3], scalar1=1.0 / 256, scalar2=EPS,
                                                               op0=ALU.mult, op1=ALU.add), reads=[("t", h)], writes=[("t", h)])
                    S.op("act", lambda e, o0=o0: e.activation(out=sm[:, o0 + 2:o0 + 3], in_=sm[:, o0 + 2:o0 + 3], func=AF.Sqrt), reads=[("t", h)], writes=[("t", h)])
                    S.op("dve", lambda e, o0=o0: e.reciprocal(out=sm[:, o0 + 2:o0 + 3], in_=sm[:, o0 + 2:o0 + 3]), reads=[("t", h)], writes=[("t", h)])
                    S.op("dve", lambda e, o0=o0: e.tensor_tensor(out=sm[:, o0 + 3:o0 + 4], in0=sm[:, o0 + 2:o0 + 3], in1=sm[:, o0:o0 + 1], op=ALU.mult),
                         reads=[("t", h), ("r", h)], writes=[("fac", h)])
                    S.op("dve", lambda e, o0=o0, hs=hs: e.scalar_tensor_tensor(out=htmp[:, :], in0=banks[2][:, 0:256], scalar=sm[:, o0 + 3:o0 + 4], in1=anb[:, hs],
                                                                            op0=ALU.mult, op1=ALU.mult), reads=[("fac", h), "anb"], writes=[BK(2), "htmp"])
                    S.op("pool", lambda e, hs=hs, mk=mk, b2=b2: e.tensor_tensor(out=mtok[mk][:, hs], in0=htmp[:, :], in1=ot[b2][:, hs], op=ALU.mult),
                         reads=["htmp", ("ot", b2)], writes=[("mtok", mk, h)])
            if not state_only:
                for g in range(4):
                    gs = slice(g * 256, (g + 1) * 256)
                    S.op("pe", lambda e, g=g, gs=gs, b2=b2: e.matmul(banks[5][:, 0:256], lhsT=spTm[:, g, :], rhs=zt[b2][:, gs], start=True, stop=True),
                         reads=[("spTm", g), ("zt", b2)], writes=[BK(5)])
                    S.op("dve", lambda e, g=g, gs=gs, b2=b2, mk=mk: e.scalar_tensor_tensor(out=mtok[mk][:, 1024 + g * 256:1024 + (g + 1) * 256], in0=banks[5][:, 0:256],
                                                                                        scalar=sbias[:, g:g + 1], in1=ut[b2][:, gs], op0=ALU.add, op1=ALU.mult),
                         reads=["sbias", ("ut", b2)], writes=[BK(5), ("mtok", mk, 4 + g)])
                psTv = banks[6][:, :].bitcast(BF16).rearrange("p (h t) -> p h t", h=8)
                for half in range(2):
                    for c8 in range(8):
                        fc = half * 8 + c8
                        S.op("pe", lambda e, c8=c8, fc=fc, mk=mk, psTv=psTv: e.transpose(out=psTv[:, c8, :], in_=mtok[mk][:, fc * 128:(fc + 1) * 128], identity=ident[:, :]),
                             reads=[("mtok", mk, fc // 2), "ident"], writes=[BK(6)])
                    S.op("act", lambda e, half=half, psTv=psTv: e.activation(out=mTs[half][:, :, :], in_=psTv[:, :, :], func=AF.Copy),
                         reads=[], writes=[BK(6), ("mTs", half)])
                    S.dma("sp", mT.rearrange("(c p) t -> p c t", p=128)[:, half * 8:(half + 1) * 8, tok], mTs[half][:, :, :], reads=[("mTs", half)],
                          writes=[("dram", "mT", tt, half)], key=("st", half))
        if state_only:
            for h in range(4):
                for dc in range(2):
                    S.dma("sp", Sout[h * 2 + dc, :, :], C[h][:, dc, :], reads=[("C", h, dc)], writes=[("dram", "Sout", h, dc)], key=("st", dc))
            S.dma("sp", Bout[:, :], sm[:, 48:52], reads=["btot"], writes=[("dram", "Bout")], key=("st", 2))
        S.emit()
    return nc


def a_shared_inputs(inp):
    import ml_dtypes
    tri = np.triu(np.ones((128, 128), dtype=np.float32))
    return {
        "gn": np.ascontiguousarray(inp["mix_norm"][0].reshape(NDC, 128).T),
        "w": np.ascontiguousarray(inp["even_w_in"][0]),
        "gb": np.ascontiguousarray(np.tile(np.concatenate([inp["even_i_bias"][0], inp["even_f_bias"][0]])[None, :], (128, 1))),
        "tri": tri,
        "anb": np.ascontiguousarray(np.tile(inp["even_a_norm"][0].reshape(1, 1024), (128, 1))),
        "bnb": np.ascontiguousarray(np.tile(inp["even_b_norm"][0].reshape(1, 1024), (128, 1))),
        "spT": np.ascontiguousarray(inp["even_spatial"][0].transpose(2, 0, 1)),
        "sbias": np.ascontiguousarray(inp["even_spatial_bias"][0].T),
        "ident": np.eye(128, dtype=np.float32).astype(ml_dtypes.bfloat16),
    }


A1_KEYS = ("gn", "w", "gb", "tri")


def a2_state_inputs(j, Sb, Bb):
    Msel = np.zeros((128, 4), np.float32)
    Tsel = np.zeros((128, 16), np.float32)
    for i in range(4):
        if i < j:
            Msel[:, i] = 1.0
            for l in range(4):
                if i < l < j:
                    Tsel[:, 4 * i + l] = 1.0
    return {"Sall": np.ascontiguousarray(np.stack(Sb)), "Ball": np.ascontiguousarray(np.concatenate(Bb, axis=1)),
            "Msel": Msel, "Tsel": Tsel}


_NC_CACHE = {}


def _get(name, fn):
    if name not in _NC_CACHE:
        _NC_CACHE[name] = fn()
    return _NC_CACHE[name]


def _run(nc, maps):
    return run_bass_kernel_spmd(nc, maps, core_ids=list(range(8))).results


def _halo_cols(cols_by_core, c):
    cur = cols_by_core[c]
    out = np.zeros((cur.shape[0], cur.shape[1] + 2), cur.dtype)
    out[:, 2:] = cur
    if c % 4 != 0:
        out[:, :2] = cols_by_core[c - 1][:, -2:]
    return out


def kernel(x, mix_norm, even_w_in, even_i_bias, even_f_bias, even_a_norm, even_b_norm, even_spatial,
           even_spatial_bias, even_w_out, odd_w_in, odd_q_norm, odd_k_norm, odd_idx_ln_g, odd_idx_ln_b,
           odd_w_out, ffn_norm, ffn_w_up, ffn_conv_w, ffn_conv_b, ffn_w_down):
    import ml_dtypes
    bf = ml_dtypes.bfloat16
    inp = dict(mix_norm=mix_norm, even_w_in=even_w_in, even_i_bias=even_i_bias, even_f_bias=even_f_bias,
               even_a_norm=even_a_norm, even_b_norm=even_b_norm, even_spatial=even_spatial,
               even_spatial_bias=even_spatial_bias, odd_w_in=odd_w_in, odd_q_norm=odd_q_norm, odd_k_norm=odd_k_norm,
               odd_idx_ln_g=odd_idx_ln_g, odd_idx_ln_b=odd_idx_ln_b)
    inp = {k: np.asarray(v, dtype=np.float32) for k, v in inp.items()}
    x = np.asarray(x, dtype=np.float32)
    xT = [np.ascontiguousarray(x[c // 4, (c % 4) * T:(c % 4 + 1) * T].T) for c in range(8)]
    sh = a_shared_inputs(inp)
    w1 = np.ascontiguousarray(np.concatenate([inp["even_w_in"][0][:, 1024:3072], inp["even_w_in"][0][:, 4096:4160]], axis=1))
    r1 = _run(_get("a1", lambda: build_a(True)), [dict(gn=sh["gn"], gb=sh["gb"], tri=sh["tri"], w=w1, xT=xT[c]) for c in range(8)])
    maps = []
    for c in range(8):
        b0_ = (c // 4) * 4
        maps.append(dict(sh, xT=xT[c], **a2_state_inputs(c % 4, [r1[b0_ + i]["Sout"] for i in range(4)],
                                                         [r1[b0_ + i]["Bout"] for i in range(4)])))
    r2 = _run(_get("a2", lambda: build_a(False)), maps)
    del r1, maps
    mT0 = [r2[c]["mT"] for c in range(8)]
    nc_of = _get("of", build_of)

    def run_of(L, hT_list, mT_list, wo):
        shf = ffn_inputs(None, np.asarray(ffn_norm[L], np.float32), np.asarray(ffn_w_up[L], np.float32),
                         np.asarray(ffn_conv_w[L], np.float32), np.asarray(ffn_conv_b[L], np.float32),
                         np.asarray(ffn_w_down[L], np.float32))
        shf["wo"] = np.ascontiguousarray(np.asarray(wo, np.float32))
        maps = [dict(shf, hT=_halo_cols(hT_list, c), mT=_halo_cols(mT_list, c)) for c in range(8)]
        return [r["outT"] for r in _run(nc_of, maps)]

    h1T = run_of(0, xT, mT0, even_w_out[0])
    del r2, mT0
    shb = b0_inputs(np.zeros((1, 1), np.float32), np.arange(T), inp)
    maps = []
    for c in range(8):
        pos = np.arange((c % 4) * T, (c % 4 + 1) * T)
        maps.append(dict(shb, hT=h1T[c], cs=rope_tables(pos, 128), cs32=rope_tables(pos, 32)))
    r3 = _run(_get("b0", build_b0), maps)
    maps = []
    for c in range(8):
        b0_ = (c // 4) * 4
        proj = {k: np.concatenate([r3[b0_ + i][k] for i in range(4)], axis=(0 if k in ("v", "iw") else -1))
                for k in ("qT", "kT", "v", "iqT", "ikT", "iw")}
        maps.append(b_inputs(proj, c % 4, fold_tiles(c % 4)))
    r4 = _run(_get("b", build_b), maps)
    del r3, maps
    obT = [np.zeros((D, T), bf) for _ in range(8)]
    for c in range(8):
        b0_ = (c // 4) * 4
        for i, t in enumerate(fold_tiles(c % 4)):
            dst = obT[b0_ + t // 16]
            blk = r4[c]["obq"][i]
            dst[:, (t % 16) * 128:(t % 16 + 1) * 128] = blk.transpose(1, 0, 2).reshape(D, 128)
    del r4
    outT = run_of(1, h1T, obT, odd_w_out[0])
    y = np.empty((2, 4 * T, D), np.float32)
    for c in range(8):
        y[c // 4, (c % 4) * T:(c % 4 + 1) * T] = outT[c].T
    return y
```
